# Optimizing a Trainium2 kernel written in Bass

```python
import math
import jax, jax.numpy as jnp
from jax import lax
import numpy as np

D_MODEL = 1024
BATCH = 8
SEQ = 2048
DEPTH = 4
DEC_BATCH = 128
DEC_SEQ = 4
PAST_LEN = 8192
PAGE_SIZE = 128

N_META = 16
HEAD_DIM = 64
N_Q_HEADS = 8
N_KV_HEADS = 2
GQA_GROUP = N_Q_HEADS // N_KV_HEADS
ATTN_W = N_Q_HEADS * HEAD_DIM
KV_W = N_KV_HEADS * HEAD_DIM
LRU_W = D_MODEL - ATTN_W
LRU_BLOCKS = 8
LRU_BLOCK_W = LRU_W // LRU_BLOCKS
CONV_W = 4
LRU_C = 8.0
WINDOW = 128
BLOCK = 128
ROPE_THETA = 10000.0
D_FF = -(-8 * D_MODEL // (3 * 256)) * 256
IN_W = ATTN_W + 2 * KV_W + 2 * LRU_W
MIX_W = ATTN_W + LRU_W
EPS = 1e-6

kernel_name = "hymba_griffin_swa_sink_step"


def rms_norm(x, g):
    xf = x.astype(jnp.float32)
    y = xf * lax.rsqrt(jnp.mean(xf * xf, axis=-1, keepdims=True) + EPS)
    return (y * g.astype(jnp.float32)).astype(x.dtype)


def rope(x, pos):
    half = HEAD_DIM // 2
    inv = ROPE_THETA ** (-jnp.arange(half, dtype=jnp.float32) / half)
    ang = pos.astype(jnp.float32)[:, None] * inv[None, :]
    cos = jnp.cos(ang)[:, None, :]
    sin = jnp.sin(ang)[:, None, :]
    xf = x.astype(jnp.float32)
    x1, x2 = xf[..., :half], xf[..., half:]
    out = jnp.concatenate([x1 * cos - x2 * sin, x2 * cos + x1 * sin], axis=-1)
    return out.astype(x.dtype)


def sink_softmax(s, sink):
    m = jnp.maximum(jnp.max(s, axis=-1, keepdims=True), sink)
    p = jnp.exp(s - m)
    return p / (jnp.sum(p, axis=-1, keepdims=True) + jnp.exp(sink - m))


def swa_prompt(q, k, v, sinks):
    B, L = q.shape[0], q.shape[1]
    pad = (-L) % BLOCK
    padw = ((0, 0), (pad, 0), (0, 0), (0, 0))
    qp, kp, vp = jnp.pad(q, padw), jnp.pad(k, padw), jnp.pad(v, padw)
    nb = (L + pad) // BLOCK
    qb = qp.reshape(B, nb, BLOCK, N_KV_HEADS, GQA_GROUP, HEAD_DIM)
    kb = kp.reshape(B, nb, BLOCK, N_KV_HEADS, HEAD_DIM)
    vb = vp.reshape(B, nb, BLOCK, N_KV_HEADS, HEAD_DIM)
    shift = ((0, 0), (1, 0), (0, 0), (0, 0), (0, 0))
    kk = jnp.concatenate([jnp.pad(kb, shift)[:, :-1], kb], axis=2)
    vv = jnp.concatenate([jnp.pad(vb, shift)[:, :-1], vb], axis=2)
    s = jnp.einsum('bnqkgd,bnskd->bnkgqs', qb, kk,
                   preferred_element_type=jnp.float32) * (HEAD_DIM ** -0.5)
    i = jnp.arange(BLOCK)[:, None]
    j = jnp.arange(2 * BLOCK)[None, :]
    n = jnp.arange(nb)[:, None, None]
    kidx = (n - 1) * BLOCK + j - pad
    rel = BLOCK + i - j
    mask = (kidx >= 0) & (rel >= 0) & (rel < WINDOW)
    s = jnp.where(mask[None, :, None, None], s, -jnp.inf)
    sink = sinks.astype(jnp.float32).reshape(N_KV_HEADS, GQA_GROUP)[:, :, None, None]
    p = sink_softmax(s, sink)
    o = jnp.einsum('bnkgqs,bnskd->bnqkgd', p.astype(v.dtype), vv)
    return o.reshape(B, L + pad, ATTN_W)[:, pad:]


def swa_sample(q, k_new, v_new, k_buf, v_buf, sinks, pos):
    B, S = q.shape[0], q.shape[1]
    kk = jnp.concatenate([k_buf.astype(k_new.dtype), k_new], axis=1)
    vv = jnp.concatenate([v_buf.astype(v_new.dtype), v_new], axis=1)
    kpos = jnp.concatenate([pos[0] - WINDOW + jnp.arange(WINDOW, dtype=pos.dtype), pos])
    rel = pos[:, None] - kpos[None, :]
    mask = (kpos[None, :] >= 0) & (rel >= 0) & (rel < WINDOW)
    qg = q.reshape(B, S, N_KV_HEADS, GQA_GROUP, HEAD_DIM)
    s = jnp.einsum('bqkgd,bskd->bkgqs', qg, kk,
                   preferred_element_type=jnp.float32) * (HEAD_DIM ** -0.5)
    s = jnp.where(mask, s, -jnp.inf)
    sink = sinks.astype(jnp.float32).reshape(N_KV_HEADS, GQA_GROUP)[:, :, None, None]
    p = sink_softmax(s, sink)
    o = jnp.einsum('bkgqs,bskd->bqkgd', p.astype(vv.dtype), vv)
    return o.reshape(B, S, ATTN_W), kk[:, -WINDOW:], vv[:, -WINDOW:]


def rg_lru(xb, conv_prefix, h0, conv_w, conv_b, w_a, b_a, w_i, b_i, lam):
    B, L = xb.shape[0], xb.shape[1]
    xc = jnp.concatenate([conv_prefix.astype(xb.dtype), xb], axis=1)
    u = conv_b.astype(xb.dtype)
    for j in range(CONV_W):
        u = u + xc[:, j:j + L] * conv_w[j].astype(xb.dtype)
    new_conv = xc[:, -(CONV_W - 1):]
    ub = u.reshape(B, L, LRU_BLOCKS, LRU_BLOCK_W)
    r = jax.nn.sigmoid((jnp.einsum('blhi,hij->blhj', ub, w_a).reshape(B, L, LRU_W)
                        + b_a).astype(jnp.float32))
    ig = jax.nn.sigmoid((jnp.einsum('blhi,hij->blhj', ub, w_i).reshape(B, L, LRU_W)
                         + b_i).astype(jnp.float32))
    log_a = LRU_C * r * jax.nn.log_sigmoid(lam.astype(jnp.float32))
    a = jnp.exp(log_a)
    bt = jnp.sqrt(-jnp.expm1(2.0 * log_a)) * ig * u.astype(jnp.float32)

    def step(h, ab):
        a_t, b_t = ab
        h = a_t * h + b_t
        return h, h

    hT, hs = lax.scan(step, h0.astype(jnp.float32),
                      (jnp.swapaxes(a, 0, 1), jnp.swapaxes(bt, 0, 1)))
    return jnp.swapaxes(hs, 0, 1).astype(xb.dtype), hT, new_conv


def decoder_stack(x, pos, k_bufs, v_bufs, h0s, conv0s,
                  pre_mix_norm, w_in, sinks, conv_w, conv_b, w_a, b_a, w_i, b_i, lam,
                  attn_out_norm, lru_out_norm, w_out, post_mix_norm,
                  pre_ffn_norm, w_gate, w_up, w_down, post_ffn_norm):
    B, L = x.shape[0], x.shape[1]
    splits = [ATTN_W, ATTN_W + KV_W, ATTN_W + 2 * KV_W, ATTN_W + 2 * KV_W + LRU_W]
    nks, nvs, nhs, ncs = [], [], [], []
    for l in range(DEPTH):
        hn = rms_norm(x, pre_mix_norm[l])
        z = hn @ w_in[l]
        q, k, v, xb, gate = jnp.split(z, splits, axis=-1)
        q = rope(q.reshape(B, L, N_Q_HEADS, HEAD_DIM), pos)
        k = rope(k.reshape(B, L, N_KV_HEADS, HEAD_DIM), pos)
        v = v.reshape(B, L, N_KV_HEADS, HEAD_DIM)
        if k_bufs is None:
            o_att = swa_prompt(q, k, v, sinks[l])
            nk, nv = k[:, -WINDOW:], v[:, -WINDOW:]
        else:
            o_att, nk, nv = swa_sample(q, k, v, k_bufs[l], v_bufs[l], sinks[l], pos)
        h_seq, hT, nconv = rg_lru(xb, conv0s[l], h0s[l], conv_w[l], conv_b[l],
                                  w_a[l], b_a[l], w_i[l], b_i[l], lam[l])
        o_lru = h_seq * jax.nn.gelu(gate, approximate=True)
        merged = jnp.concatenate([rms_norm(o_att, attn_out_norm[l]),
                                  rms_norm(o_lru, lru_out_norm[l])], axis=-1)
        x = x + rms_norm(merged @ w_out[l], post_mix_norm[l])
        hf = rms_norm(x, pre_ffn_norm[l])
        f = (jax.nn.silu(hf @ w_gate[l]) * (hf @ w_up[l])) @ w_down[l]
        x = x + rms_norm(f, post_ffn_norm[l])
        nks.append(nk); nvs.append(nv); nhs.append(hT); ncs.append(nconv)
    return x, jnp.stack(nks), jnp.stack(nvs), jnp.stack(nhs), jnp.stack(ncs)


def setup_inputs(seed: int = 0) -> dict:
    key = jax.random.key(seed)
    ks = jax.random.split(key, 32)
    f32 = jnp.float32
    nrm = lambda k, shape, s: jax.random.normal(k, shape, f32) * s
    gain = lambda k, shape: 1.0 + 0.01 * jax.random.normal(k, shape, f32)
    u = jax.random.uniform(ks[0], (DEPTH, LRU_W), f32, 0.9, 0.999)
    a0 = u ** (1.0 / LRU_C)
    lam = jnp.log(a0) - jnp.log1p(-a0)
    return {
        "x_prompt": nrm(ks[1], (BATCH, SEQ, D_MODEL), 1.0),
        "x_sample": nrm(ks[2], (DEC_BATCH, DEC_SEQ, D_MODEL), 1.0),
        "cache_k": nrm(ks[3], (DEPTH, DEC_BATCH, WINDOW, N_KV_HEADS, HEAD_DIM), 1.0),
        "cache_v": nrm(ks[4], (DEPTH, DEC_BATCH, WINDOW, N_KV_HEADS, HEAD_DIM), 1.0),
        "state_h": nrm(ks[5], (DEPTH, DEC_BATCH, LRU_W), 0.5),
        "state_conv": nrm(ks[6], (DEPTH, DEC_BATCH, CONV_W - 1, LRU_W), 1.0),
        "meta_tokens": nrm(ks[7], (N_META, D_MODEL), 1.0),
        "pre_mix_norm": gain(ks[8], (DEPTH, D_MODEL)),
        "w_in": nrm(ks[9], (DEPTH, D_MODEL, IN_W), D_MODEL ** -0.5),
        "sinks": nrm(ks[10], (DEPTH, N_Q_HEADS), 0.5),
        "conv_w": nrm(ks[11], (DEPTH, CONV_W, LRU_W), CONV_W ** -0.5),
        "conv_b": nrm(ks[12], (DEPTH, LRU_W), 0.01),
        "w_a": nrm(ks[13], (DEPTH, LRU_BLOCKS, LRU_BLOCK_W, LRU_BLOCK_W), LRU_BLOCK_W ** -0.5),
        "b_a": nrm(ks[14], (DEPTH, LRU_W), 0.01),
        "w_i": nrm(ks[15], (DEPTH, LRU_BLOCKS, LRU_BLOCK_W, LRU_BLOCK_W), LRU_BLOCK_W ** -0.5),
        "b_i": nrm(ks[16], (DEPTH, LRU_W), 0.01),
        "lam": lam,
        "attn_out_norm": gain(ks[17], (DEPTH, ATTN_W)),
        "lru_out_norm": gain(ks[18], (DEPTH, LRU_W)),
        "w_out": nrm(ks[19], (DEPTH, MIX_W, D_MODEL), MIX_W ** -0.5),
        "post_mix_norm": gain(ks[20], (DEPTH, D_MODEL)),
        "pre_ffn_norm": gain(ks[21], (DEPTH, D_MODEL)),
        "w_gate": nrm(ks[22], (DEPTH, D_MODEL, D_FF), D_MODEL ** -0.5),
        "w_up": nrm(ks[23], (DEPTH, D_MODEL, D_FF), D_MODEL ** -0.5),
        "w_down": nrm(ks[24], (DEPTH, D_FF, D_MODEL), D_FF ** -0.5),
        "post_ffn_norm": gain(ks[25], (DEPTH, D_MODEL)),
    }


def reference(x_prompt, x_sample, cache_k, cache_v, state_h, state_conv, meta_tokens,
              pre_mix_norm, w_in, sinks, conv_w, conv_b, w_a, b_a, w_i, b_i, lam,
              attn_out_norm, lru_out_norm, w_out, post_mix_norm,
              pre_ffn_norm, w_gate, w_up, w_down, post_ffn_norm):
    weights = (pre_mix_norm, w_in, sinks, conv_w, conv_b, w_a, b_a, w_i, b_i, lam,
               attn_out_norm, lru_out_norm, w_out, post_mix_norm,
               pre_ffn_norm, w_gate, w_up, w_down, post_ffn_norm)
    B = x_prompt.shape[0]
    meta = jnp.broadcast_to(meta_tokens.astype(x_prompt.dtype)[None], (B, N_META, D_MODEL))
    xp = jnp.concatenate([meta, x_prompt], axis=1)
    pos_p = jnp.arange(xp.shape[1], dtype=jnp.int32)
    h0 = jnp.zeros((DEPTH, B, LRU_W), jnp.float32)
    c0 = jnp.zeros((DEPTH, B, CONV_W - 1, LRU_W), x_prompt.dtype)
    yp, prompt_k, prompt_v, prompt_h, prompt_conv = decoder_stack(
        xp, pos_p, None, None, h0, c0, *weights)
    y_prompt = yp[:, N_META:]
    pos_s = PAST_LEN + jnp.arange(x_sample.shape[1], dtype=jnp.int32)
    y_sample, sample_k, sample_v, sample_h, sample_conv = decoder_stack(
        x_sample, pos_s, cache_k, cache_v, state_h, state_conv, *weights)
    return (y_prompt, y_sample, prompt_k, prompt_v, prompt_h, prompt_conv,
            sample_k, sample_v, sample_h, sample_conv)
```

```python
import contextlib
import numpy as np
import concourse.bass as bass
import concourse.mybir as mybir
from concourse.bass_utils import run_bass_kernel_spmd

F32 = mybir.dt.float32
BF16 = mybir.dt.bfloat16
AF = mybir.ActivationFunctionType
ALU = mybir.AluOpType

NCORES = 8
DEPTH = 4
D = 1024
SEQ = 2048
NMETA = 16
NSMP = 64
NSEQ = 16
DFF = 2816
NFF = 22
EPS = 1e-6
PAST = 8192
TA = 64 + 16 + 1024
TB = 1024
TMAX = TA
NRING = 8
NFILL = 4
AHEAD = 5
NU_IN = 21
IN_ORDER = [13, 14, 15, 16, 17, 18, 19, 20, 0, 4, 1, 5, 2, 6, 3, 7, 8, 10, 9, 11, 12]


class Sched:
    def __init__(self, nc, stack, n_dma=10):
        self.nc = nc
        self.engs = {"pe": nc.tensor, "act": nc.scalar, "dve": nc.vector,
                     "pool": nc.gpsimd, "sp": nc.sync}
        self.sem, self.cnt, self.known = {}, {}, {}
        for n in self.engs:
            self.sem[n] = stack.enter_context(nc.semaphore("s_" + n))
            self.cnt[n] = 0
            self.known[n] = {}
        self.n_dma = n_dma
        for pre in ("d", "w"):
            for i in range(n_dma):
                n = "%s%d" % (pre, i)
                self.sem[n] = stack.enter_context(nc.semaphore("s_" + n))
                self.cnt[n] = 0
        self.buf = {}
        self.dma_rr = {"d": 0, "w": 0}

    def _need(self, eng, deps):
        e = self.engs[eng]
        kn = self.known[eng]
        for tl, v in deps.items():
            if v <= 0 or kn.get(tl, 0) >= v:
                continue
            if tl == eng:
                if eng == "pe" or eng == "sp" or v > self.cnt[eng]:
                    continue
            else:
                assert v <= self.cnt[tl], ("wait on un-issued event", eng, tl, v, self.cnt[tl])
            e.wait_ge(self.sem[tl], v)
            kn[tl] = v

    def _collect(self, reads, writes, eng=None):
        deps = {}
        for k in reads:
            st = self.buf.get(k)
            if st and st[0] is not None:
                tl, v = st[0]
                if deps.get(tl, 0) < v:
                    deps[tl] = v
            if st and isinstance(k, tuple) and k[0] == "P":
                for tl, v in st[1].items():
                    if tl != eng and deps.get(tl, 0) < v:
                        deps[tl] = v
        for k in writes:
            st = self.buf.get(k)
            if st:
                if st[0] is not None:
                    tl, v = st[0]
                    if deps.get(tl, 0) < v:
                        deps[tl] = v
                for tl, v in st[1].items():
                    if deps.get(tl, 0) < v:
                        deps[tl] = v
        return deps

    def _record(self, ev, reads, writes):
        tl, v = ev
        for k in reads:
            st = self.buf.get(k)
            if st is None:
                st = self.buf[k] = [None, {}]
            if st[1].get(tl, 0) < v:
                st[1][tl] = v
        for k in writes:
            self.buf[k] = [ev, {}]

    def op(self, eng, fn, reads=(), writes=(), inc=True):
        self._need(eng, self._collect(reads, writes, eng))
        ins = fn(self.engs[eng])
        ticket = self.cnt[eng] + 1
        if inc:
            ins.then_inc(self.sem[eng], 1)
            self.cnt[eng] = ticket
        self._record((eng, ticket), reads, writes)
        return ins

    def dma(self, q, out, in_, reads=(), writes=()):
        pre = "w" if q == "pool" else "d"
        ch = "%s%d" % (pre, self.dma_rr[pre])
        self.dma_rr[pre] = (self.dma_rr[pre] + 1) % self.n_dma
        deps = self._collect(reads, writes)
        if self.cnt[ch] > 0:
            deps[ch] = max(deps.get(ch, 0), self.cnt[ch])
        self._need(q, deps)
        ins = self.engs[q].dma_start(out=out, in_=in_)
        self.cnt[ch] += 16
        ins.then_inc(self.sem[ch], 16)
        self._record((ch, self.cnt[ch]), reads, writes)
        return ins

    def barrier(self):
        for eng in self.engs:
            self.finish(eng)

    def finish(self, eng="sp"):
        deps = {tl: c for tl, c in self.cnt.items() if c > 0 and tl != eng}
        self._need(eng, deps)


def seg_bounds(s):
    if s == 0:
        return [(0, 64), (64, 80)] + [(80 + 128 * n, 80 + 128 * (n + 1)) for n in range(8)]
    return [(128 * n, 128 * (n + 1)) for n in range(8)]


def tile_list(s):
    if s == 0:
        return [(0, 80), (80, 592), (592, 1104)]
    return [(0, 512), (512, 1024)]


def segs_of(s, lo, hi):
    return [i for i, (a, b) in enumerate(seg_bounds(s)) if a < hi and b > lo]


class _Stop(Exception):
    pass


DBG_STOP = None


def chk(name):
    if DBG_STOP is not None and DBG_STOP == name:
        raise _Stop()


def build_program():
    nc = bass.Bass("TRN2", target_bir_lowering=False)

    def din(name, shape, dt=F32):
        return nc.dram_tensor(name, list(shape), dt, kind="ExternalInput").ap()

    def dout(name, shape, dt=F32):
        return nc.dram_tensor(name, list(shape), dt, kind="ExternalOutput").ap()

    xT_in = din("xT_in", [8, 128, TA + TB])
    w_in_u = din("w_in_u", [DEPTH, NU_IN, 128, 8, 128])
    w_out_u = din("w_out_u", [DEPTH, 8, 128, 8, 128])
    w_gate_u = din("w_gate_u", [DEPTH, NFF, 128, 8, 128])
    w_up_u = din("w_up_u", [DEPTH, NFF, 128, 8, 128])
    w_down_u = din("w_down_u", [DEPTH, 2, 8, 128, 11, 128])
    w_lru = din("w_lru", [DEPTH, 128, 2, 4, 128])
    prm_in = din("prm", [128, DEPTH, 88])
    rope_in = din("rope", [2, 128, 2, TMAX])
    msk_in = din("msk", [128, 5, 128])
    ident_in = din("ident", [128, 128])
    cache_k = din("cache_k", [DEPTH, NSEQ, 128, 128])
    cache_v = din("cache_v", [DEPTH, NSEQ, 128, 128])
    sth_in = din("sth", [DEPTH, 128, 4, NSEQ])
    stc_in = din("stc", [DEPTH, 128, 4, NSEQ, 3])

    yT = dout("yT", [8, 128, NSMP + SEQ])
    koutT = dout("koutT", [DEPTH, 2, 128, NSMP + 128])
    vout = dout("vout", [DEPTH, NSMP + 128, 128])
    hout_p = dout("hout_p", [DEPTH, 128, 4])
    hout_s = dout("hout_s", [DEPTH, 128, 4, NSEQ])
    cout_p = dout("cout_p", [DEPTH, 128, 4, 3])
    cout_s = dout("cout_s", [DEPTH, 128, 4, 3 * NSEQ])
    sk_old = dout("sk_old", [DEPTH, NSEQ, 124, 128])
    sv_old = dout("sv_old", [DEPTH, NSEQ, 124, 128])

    with contextlib.ExitStack() as st:
        S = Sched(nc, st)

        def sb(name, shape, dt=F32):
            return st.enter_context(nc.sbuf_tensor(name, list(shape), dt))

        x = sb("x", [128, 8, TMAX])
        act = sb("act", [128, 8, TMAX], BF16)
        F1 = sb("F1", [128, 8, TMAX])
        H = sb("H", [128, 11, TMAX], BF16)
        gate = F1
        qT = H
        KT0 = 4
        vsb_flat = H[:, 6:9, :].rearrange("p a b -> p (a b)")[:, 0:10 * 256]
        vsb = vsb_flat.rearrange("p (n g d) -> p n g d", n=10, g=2)
        wsl = [sb("wsl%d" % i, [128, 11 * 128], BF16) for i in range(NRING)]
        rope = sb("rope_sb", [128, 2, TMAX])
        prm = sb("prm_sb", [128, DEPTH, 88])
        c8 = sb("c8", [128, DEPTH, 24])
        q25 = sb("q25", [128, 1])
        esink = sb("esink", [128, DEPTH, 16])
        msk = sb("msk_sb", [128, 5, 128], BF16)
        ident = sb("ident_sb", [128, 128], BF16)
        ones = sb("ones", [128, 128], BF16)
        one1 = sb("one1", [128, 1])
        epsc = sb("epsc", [128, 2])
        wl = [sb("wl0", [128, 2, 4, 128], BF16)] * 2
        kT_c = sb("kT_c", [128, DEPTH, 2, 128], BF16)
        v_c = sb("v_c", [128, DEPTH, 2, 128], BF16)
        h_c = sb("h_c", [128, DEPTH, 4])
        xbc = sb("xbc", [128, DEPTH, 4, 3])
        sq = [sb("sq%d" % i, [128, 512], BF16) for i in range(4)]
        rstd = [sb("rstd%d" % i, [128, 512]) for i in range(3)]
        tmpa = [sb("tmpa%d" % i, [128, 512]) for i in range(2)]
        tmpb = [sb("tmpb%d" % i, [128, 512]) for i in range(2)]
        kv32 = sb("kv32", [128, 2, NSMP + 128])
        v32 = sb("v32", [128, 2, 128])
        arena = sb("arena", [128, 6912])

        def av(off, words, dt=F32, p0=0, p1=128):
            a = arena[p0:p1, off:off + words]
            return a if dt == F32 else a.bitcast(BF16)
        pT = [av(o_, 512, BF16).rearrange("p (k h c q) -> p k h c q", h=2, k=2, c=2) for o_ in (0, 512, 5376)]
        den = [av(1024 + 512 * i, 512) for i in range(2)]
        oT = [av(o_, 512).rearrange("p (c q) -> p c q", c=4) for o_ in (2048, 2560, 5888)]
        sqa = av(3072, 256, BF16).rearrange("p (c q) -> p c q", c=4)
        pTc = av(3328, 256, BF16)
        pTn = av(3584, 256, BF16, 0, 64)
        kcA = [av(3840, 256, BF16).rearrange("p (b x) -> p b x", b=4)] * 2
        kcB = [av(3840 + 256, 256, BF16).rearrange("p (b x) -> p b x", b=4)] * 2
        kTc = [av(3840 + 512, 512, BF16).rearrange("p (b a j) -> p b a j", b=4, a=2)] * 2
        vc = [av(3840 + 1024, 512, BF16).rearrange("p (b g r d) -> p b g r d", b=4, g=2, r=2)] * 2
        lt = [[av(1536 * i + 512 * j, 512) for j in range(3)] for i in range(2)]
        u32 = [av(3072 + 512 * i, 512) for i in range(3)]
        u16 = [av(4608 + 256 * i, 256, BF16) for i in range(3)]
        fsc = sb("fsc", [128, 2])
        esr = av(6400, 512, BF16, 0, 1).rearrange("p (g x) -> p g x", g=2)
        xs = sb("xs", [128, 4, NSEQ, 7])
        h0s = sb("h0s", [128, 4, NSEQ])
        ATT_KEYS = ([("pT", i, h) for i in range(3) for h in range(2)] + [("den", i) for i in range(2)]
                    + [("oT", id(oT[i]), g) for i in range(3) for g in range(2)] + ["sqa"]
                    + [(nm, i) for nm in ("kcA", "kcB", "kTc", "vc") for i in range(2)]
                    + [("pTc", h, g) for h in range(2) for g in range(4)] + [("pTn", h) for h in range(2)] + ["esr"])
        LRU_KEYS = ([(nm, i) for nm in ("lt1", "lt2", "lt3") for i in range(2)] + [(nm, i) for nm in ("u32", "u16") for i in range(3)])

        def fence(wait_keys, set_keys):
            S.op("pool", lambda e: e.memset(fsc[:, 0:1], 0.0), writes=list(wait_keys) + list(set_keys))

        P = [st.enter_context(nc.psum_tensor("P%d" % i, [128, 512], F32)) for i in range(8)]
        PT = P[7][:].bitcast(BF16)

        def pk(i):
            return ("P", i)

        S.dma("sp", prm[:], prm_in, writes=["prm"])
        S.dma("pool", msk[:], msk_in, writes=["msk"])
        S.dma("pool", ident[:], ident_in, writes=["ident"])
        S.op("pool", lambda e: e.memset(ones[:], 1.0), writes=["ones"])
        S.op("pool", lambda e: e.memset(one1[:], 1.0), writes=["one1"])
        S.op("pool", lambda e: e.memset(epsc[:, 0:1], 1024.0 * EPS), writes=["epsc"])
        S.op("pool", lambda e: e.memset(epsc[:, 1:2], 512.0 * EPS), writes=["epsc"])
        S.op("dve", lambda e: e.tensor_scalar(prm[:, :, 0:32], prm[:, :, 0:32], float(np.sqrt(1024.0)), None, ALU.mult),
             reads=["prm"], writes=["prm"])
        S.op("dve", lambda e: e.tensor_scalar(prm[:, :, 32:40], prm[:, :, 32:40], float(np.sqrt(512.0)), None, ALU.mult),
             reads=["prm"], writes=["prm"])
        S.op("act", lambda e: e.activation(c8[:, :, 0:4], prm[:, :, 68:72], AF.Exp, scale=-1.0),
             reads=["prm"], writes=["c8"])
        S.op("act", lambda e: e.activation(c8[:, :, 0:4], c8[:, :, 0:4], AF.Ln, bias=one1[:]),
             reads=["c8", "one1"], writes=["c8"])
        S.op("dve", lambda e: e.tensor_scalar(c8[:, :, 4:8], c8[:, :, 0:4], -16.0, None, ALU.mult),
             reads=["c8"], writes=["c8b"])
        S.op("dve", lambda e: e.tensor_scalar(c8[:, :, 0:4], c8[:, :, 0:4], -8.0, None, ALU.mult),
             reads=["c8", "c8b"], writes=["c8"])
        S.op("dve", lambda e: e.tensor_scalar(c8[:, :, 8:12], c8[:, :, 0:4], 0.5, None, ALU.mult), reads=["c8"], writes=["c8c"])
        S.op("dve", lambda e: e.tensor_scalar(c8[:, :, 16:24], prm[:, :, 60:68], 0.5, None, ALU.mult), reads=["prm", "c8", "c8b", "c8c"], writes=["c8"])
        S.op("pool", lambda e: e.memset(q25[:], 0.25 + 5e-7), writes=["q25"])
        S.op("act", lambda e: e.activation(esink[:], prm[:, :, 72:88], AF.Exp), reads=["prm"], writes=["esink"])
        CONST = ["prm", "c8", "esink", "msk", "ident", "ones", "one1", "q25"]

        wstate = {"n": 0}

        def load_unit(src_ap, nk):
            i = wstate["n"]
            wstate["n"] += 1
            ws = wsl[i % NRING]
            S.dma("pool", ws[:, 0:nk * 128], src_ap.rearrange("p k n -> p (k n)"), writes=[("wsl", i % NRING)])
            return ws[:, 0:nk * 128].rearrange("p (k n) -> p k n", k=nk), ("wsl", i % NRING)

        class WStream:
            def __init__(self):
                self.plan = []
                self.loaded = []
                self.next = 0

            def add(self, src_ap, nk):
                self.plan.append((src_ap, nk))
                return len(self.plan) - 1

            def get(self, idx, ahead=AHEAD):
                while self.next <= min(max(idx + ahead, idx), len(self.plan) - 1):
                    self.loaded.append(load_unit(*self.plan[self.next]))
                    self.next += 1
                return self.loaded[idx]

        WS = WStream()
        plan_idx = {}
        for s in range(2):
            for l in range(DEPTH):
                for uidx in IN_ORDER:
                    plan_idx[("in", s, l, uidx)] = WS.add(w_in_u[l, uidx], 8)
                for m in range(8):
                    plan_idx[("out", s, l, m)] = WS.add(w_out_u[l, m], 8)
                for hf in range(2):
                    for c in range(11):
                        plan_idx[("g", s, l, hf, c)] = WS.add(w_gate_u[l, 11 * hf + c], 8)
                        plan_idx[("u", s, l, hf, c)] = WS.add(w_up_u[l, 11 * hf + c], 8)
                    for m in range(8):
                        plan_idx[("d", s, l, hf, m)] = WS.add(w_down_u[l, hf, m], 11)

        rr = {"mm": 0, "n": 0, "ev": 0, "sq": 0}

        def mm_bank():
            i = rr["mm"]
            rr["mm"] = (i + 1) % 4
            return i

        def mm_group(out_ap, out_key, terms):
            n = len(terms)
            for j, (lt, rh, rk) in enumerate(terms):
                S.op("pe", lambda e, lt=lt, rh=rh, j=j: e.matmul(out_ap, lhsT=lt, rhs=rh, start=(j == 0), stop=(j == n - 1)),
                     reads=rk, writes=[out_key], inc=(j == n - 1))

        def akeys(buf, chunks, s, lo_, hi_):
            return [(buf, c, sg) for c in chunks for sg in segs_of(s, lo_, hi_)]

        def rms_apply(s, lo_, hi_, src, src_name, nch, ch0, gcol, l, dst_fn, extra_reads=()):
            n = hi_ - lo_
            i = rr["n"]; rr["n"] = (i + 1) % 3
            rs = rstd[i]
            bank = 6
            for c in range(nch):
                qi = rr["sq"]; rr["sq"] = (qi + 1) % 4
                sqt = sq[qi]
                S.op("act", lambda e, c=c, sqt=sqt: e.activation(sqt[:, 0:n], src[:, ch0 + c, lo_:hi_], AF.Square),
                     reads=akeys(src_name, [ch0 + c], s, lo_, hi_) + list(extra_reads), writes=[("sq", qi)])
                S.op("pe", lambda e, c=c, sqt=sqt: e.matmul(P[bank][:, 0:n], lhsT=ones[:], rhs=sqt[:, 0:n], start=(c == 0), stop=(c == nch - 1)),
                     reads=[("sq", qi), "ones"], writes=[pk(bank)], inc=True)
            dd = 1024.0 if nch == 8 else 512.0
            S.op("act", lambda e: e.activation(rs[:, 0:n], P[bank][:, 0:n], AF.Ln, bias=(epsc[:, 0:1] if nch == 8 else epsc[:, 1:2])),
                 reads=[pk(bank), "epsc"], writes=[("rstd", i)])
            S.op("act", lambda e: e.activation(rs[:, 0:n], rs[:, 0:n], AF.Exp, scale=-0.5), reads=[("rstd", i)], writes=[("rstd", i)])
            for c in range(nch):
                dst_fn(c, rs[:, 0:n], ("rstd", i))

        def prenorm(l, s, lo_, hi_, gcol):
            def app(c, rs, rk):
                S.op("dve", lambda e: e.scalar_tensor_tensor(act[:, c, lo_:hi_], x[:, c, lo_:hi_], prm[:, l, gcol + c:gcol + c + 1], rs, ALU.mult, ALU.mult),
                     reads=akeys("x", [c], s, lo_, hi_) + [rk, "prm"], writes=akeys("act", [c], s, lo_, hi_))
            rms_apply(s, lo_, hi_, x, "x", 8, 0, gcol, l, app)

        def postnorm_residual(l, s, lo_, hi_, gcol):
            def app(c, rs, rk):
                n = hi_ - lo_
                j = rr["ev"]; rr["ev"] ^= 1
                S.op("dve", lambda e: e.scalar_tensor_tensor(tmpa[j][:, 0:n], F1[:, c, lo_:hi_], prm[:, l, gcol + c:gcol + c + 1], rs, ALU.mult, ALU.mult),
                     reads=akeys("F1", [c], s, lo_, hi_) + [rk, "prm"], writes=[("tmpa", j)])
                S.op("pool", lambda e: e.tensor_tensor(x[:, c, lo_:hi_], x[:, c, lo_:hi_], tmpa[j][:, 0:n], ALU.add),
                     reads=akeys("x", [c], s, lo_, hi_) + [("tmpa", j)], writes=akeys("x", [c], s, lo_, hi_))
            rms_apply(s, lo_, hi_, F1, "F1", 8, 0, gcol, l, app)

        def norm_pair(postf, pref, tiles):
            nt = len(tiles)
            for i in range(nt + 1):
                if i < nt:
                    postf(*tiles[i])
                if i >= 1:
                    pref(*tiles[i - 1])

        def tile_outer_phase(pidxs, tiles, mm_fn, postf, pref):
            last = pidxs[-1]
            ws = [WS.get(p, ahead=min(AHEAD, last - p)) for p in pidxs]
            nt = len(tiles)
            for ti, (lo_, hi_) in enumerate(tiles):
                for half in range(2):
                    for m in range(4 * half, 4 * half + 4):
                        mm_fn(m, ws[m], lo_, hi_)
                    if half == 0 and ti >= 1:
                        postf(*tiles[ti - 1])
                    if half == 1 and ti >= 2 and pref is not None:
                        pref(*tiles[ti - 2])
            postf(*tiles[nt - 1])
            if pref is not None:
                if nt >= 2:
                    pref(*tiles[nt - 2])
                pref(*tiles[nt - 1])

        def main_passes():
          for s in range(2):
              T = TA if s == 0 else TB
              tiles = tile_list(s)
              segs = seg_bounds(s)
              col0 = 0 if s == 0 else TA
              if s == 1:
                  S.barrier()
                  S.buf.clear()
              for c in range(8):
                  for (lo_, hi_) in tiles:
                      S.dma("sp", x[:, c, lo_:hi_], xT_in[c, :, col0 + lo_:col0 + hi_], writes=akeys("x", [c], s, lo_, hi_))
              S.dma("sp", rope[:, :, 0:T], rope_in[s, :, :, 0:T], writes=["rope"])

              for l in range(DEPTH):
                  lw = wl[l % 2]
                  S.dma("pool", lw[:], w_lru[l], writes=[("wl", 0)])
                  if s == 0:
                      S.dma("sp", xs[:, :, :, 0:3], stc_in[l], writes=["xs"])
                      S.dma("sp", h0s[:], sth_in[l], writes=["h0s"])
                      S.dma("sp", sk_old[l], cache_k[l, :, 4:128, :])
                      S.dma("sp", sv_old[l], cache_v[l, :, 4:128, :])

                  if l == 0:
                      for (lo_, hi_) in tiles:
                          prenorm(l, s, lo_, hi_, 0)

                  chk("A0_%d_%d" % (s, l))
                  xb_off = (3 - 64) if s == 0 else 3
                  XB0 = 4


                  def lru_pre(k, lo_, hi_, c):
                      n = hi_ - lo_
                      uu, ub = u32[k % 3], u16[k % 3]
                      uk, ubk = ("u32", k % 3), ("u16", k % 3)
                      cw = lambda j: prm[:, l, 40 + 4 * c + j:41 + 4 * c + j]
                      p0 = 0
                      if s == 0 and lo_ == 0:
                          uo = uu[:, 0:NSMP].rearrange("p (b t) -> p b t", t=4)
                          S.op("dve", lambda e: e.tensor_scalar(uo, xs[:, c, :, 0:4], cw(0), prm[:, l, 56 + c:57 + c], ALU.mult, ALU.add),
                               reads=["xs", "prm"], writes=[uk])
                          for j in range(1, 4):
                              S.op("dve", lambda e, j=j: e.scalar_tensor_tensor(uo, xs[:, c, :, j:j + 4], cw(j), uo, ALU.mult, ALU.add),
                                   reads=["xs", "prm", uk], writes=[uk])
                          p0 = NSMP
                      xl = lo_ + p0 + xb_off
                      npr = n - p0
                      xk = akeys("F1", [XB0 + c], s, xl - 3, xl + npr)
                      S.op("act", lambda e: e.activation(uu[:, p0:n], F1[:, XB0 + c, xl - 3:xl - 3 + npr], AF.Identity, scale=cw(0), bias=prm[:, l, 56 + c:57 + c]),
                           reads=xk + ["prm"], writes=[uk])
                      for j in range(1, 4):
                          S.op("dve", lambda e, j=j: e.scalar_tensor_tensor(uu[:, p0:n], F1[:, XB0 + c, xl - 3 + j:xl - 3 + j + npr], cw(j), uu[:, p0:n], ALU.mult, ALU.add),
                               reads=xk + ["prm", uk], writes=[uk])

                  def lru_cast(k, lo_, hi_, c):
                      n = hi_ - lo_
                      S.op("act", lambda e: e.copy(u16[k % 3][:, 0:n], u32[k % 3][:, 0:n]), reads=[("u32", k % 3)], writes=[("u16", k % 3)])

                  def lru_A(k, lo_, hi_, c):
                      n = hi_ - lo_
                      lt1, lt2, lt3 = lt[k % 2]
                      k1, k2, k3 = ("lt1", k % 2), ("lt2", k % 2), ("lt3", k % 2)
                      ub, ubk = u16[k % 3], ("u16", k % 3)
                      for wi, bank in ((0, 4), (1, 5)):
                          S.op("pe", lambda e, wi=wi, bank=bank: e.matmul(P[bank][:, 0:n], lhsT=lw[:, wi, c, :], rhs=ub[:, 0:n], start=True, stop=True),
                               reads=[("wl", 0), ubk], writes=[pk(bank)])
                      S.op("act", lambda e: e.activation(lt1[:, 0:n], P[4][:, 0:n], AF.Tanh, scale=0.5, bias=c8[:, l, 16 + c:17 + c]),
                           reads=[pk(4), "c8"], writes=[k1])
                      S.op("act", lambda e: e.activation(lt2[:, 0:n], P[5][:, 0:n], AF.Tanh, scale=0.5, bias=c8[:, l, 20 + c:21 + c]),
                           reads=[pk(5), "c8"], writes=[k2])
                      S.op("act", lambda e: e.activation(lt3[:, 0:n], lt1[:, 0:n], AF.Exp, scale=c8[:, l, 8 + c:9 + c], bias=c8[:, l, 8 + c:9 + c]),
                           reads=[k1, "c8"], writes=[k3])
                      if k + 1 < len(lsteps):
                          lru_cast(k + 1, *lsteps[k + 1])
                      S.op("pool", lambda e: e.tensor_tensor(lt1[:, 0:n], lt3[:, 0:n], lt3[:, 0:n], ALU.mult),
                           reads=[k3, k1], writes=[k1])
                      S.op("act", lambda e: e.activation(lt1[:, 0:n], lt1[:, 0:n], AF.Sqrt, scale=-0.25, bias=q25[:]),
                           reads=[k1, "q25"], writes=[k1])

                  def lru_B(k, lo_, hi_, c):
                      n = hi_ - lo_
                      lt1, lt2, lt3 = lt[k % 2]
                      k1, k2, k3 = ("lt1", k % 2), ("lt2", k % 2), ("lt3", k % 2)
                      uu, uk = u32[k % 3], ("u32", k % 3)
                      S.op("dve", lambda e: e.scalar_tensor_tensor(lt2[:, 0:n], lt2[:, 0:n], 1.0, uu[:, 0:n], ALU.add, ALU.mult),
                           reads=[k2, uk], writes=[k2])
                      S.op("pool", lambda e: e.tensor_tensor(lt1[:, 0:n], lt1[:, 0:n], lt2[:, 0:n], ALU.mult), reads=[k1, k2], writes=[k1])
                      if s == 0 and lo_ == 0:
                          a0 = lt3[:, 0:NSMP].rearrange("p (b t) -> p b t", t=4)[:, :, 0]
                          b0_ = lt1[:, 0:NSMP].rearrange("p (b t) -> p b t", t=4)[:, :, 0]
                          S.op("dve", lambda e: e.tensor_tensor(a0, a0, h0s[:, c, :], ALU.mult), reads=[k3, "h0s"], writes=[k3])
                          S.op("dve", lambda e: e.tensor_tensor(b0_, b0_, a0, ALU.add), reads=[k1, k3], writes=[k1])
                          S.op("dve", lambda e: e.memset(a0, 0.0), reads=[k1], writes=[k3])
                          S.op("dve", lambda e: e.memset(lt3[:, NSMP:NSMP + 1], 0.0), writes=[k3])
                          init = 0.0
                          ik = []
                      elif s == 1 and lo_ == 0:
                          init = h_c[:, l, c:c + 1]
                          ik = ["h_c"]
                      else:
                          init = hstate[c]
                          ik = ["hlast%d" % c]
                      S.op("dve", lambda e: e.tensor_tensor_scan(lt2[:, 0:n], lt3[:, 0:n], lt1[:, 0:n], init, ALU.mult, ALU.add),
                           reads=[k3, k1, k2] + ik, writes=[k2])
                      hl = sb_h[c]
                      S.op("pool", lambda e: e.tensor_copy(hl[:, 0:1], lt2[:, n - 1:n]), reads=[k2], writes=["hlast%d" % c])
                      hstate[c] = hl[:, 0:1]
                      if s == 0 and lo_ == 0:
                          S.op("pool", lambda e: e.tensor_copy(hfin[:, c, :], lt2[:, 0:NSMP].rearrange("p (b t) -> p b t", t=4)[:, :, 3]),
                               reads=[k2], writes=["hfin"])
                      S.op("pool", lambda e: e.tensor_tensor(F1[:, c, lo_:hi_], lt2[:, 0:n], F1[:, c, lo_:hi_], ALU.mult),
                           reads=[k2] + akeys("F1", [c], s, lo_, hi_), writes=akeys("F1", [c], s, lo_, hi_))

                  def lru_norm(lo_, hi_):
                      def app(c, rs, rk):
                          S.op("dve", lambda e: e.scalar_tensor_tensor(act[:, 4 + c, lo_:hi_], F1[:, c, lo_:hi_], prm[:, l, 36 + c:37 + c], rs, ALU.mult, ALU.mult),
                               reads=akeys("F1", [c], s, lo_, hi_) + [rk, "prm"], writes=akeys("act", [4 + c], s, lo_, hi_))
                      rms_apply(s, lo_, hi_, F1, "F1", 4, 0, 36, l, app)

                  if l == 0 and s == 0:
                      sb_h = [sb("hl%d" % c, [128, 1]) for c in range(4)]
                      hfin = sb("hfin", [128, 4, NSEQ])
                      hfinp = sb("hfinp", [128, 4])
                      cfin = sb("cfin", [128, 4, 3 * NSEQ])
                      cfinp = sb("cfinp", [128, 4, 3])

                  hstate = {}
                  lsteps = [(lo_, hi_, c) for (lo_, hi_) in tiles for c in range(4)]
                  lstate = {"k": 0}

                  def lru_advance(nsteps):
                      for _ in range(nsteps):
                          k = lstate["k"]
                          if k >= len(lsteps):
                              return
                          lo_, hi_, c = lsteps[k]
                          if k + 2 < len(lsteps):
                              lru_pre(k + 2, *lsteps[k + 2])
                          if k + 1 < len(lsteps):
                              lru_A(k + 1, *lsteps[k + 1])
                          lru_B(k, lo_, hi_, c)
                          lstate["k"] = k + 1

                  def inproj_fm(uidx):
                      w, wk = WS.get(plan_idx[("in", s, l, uidx)])
                      res = []
                      for (lo_, hi_) in tiles:
                          n = hi_ - lo_
                          b = mm_bank()
                          mm_group(P[b][:, 0:n], pk(b),
                                   [(w[:, kc, :], act[:, kc, lo_:hi_], [wk] + akeys("act", [kc], s, lo_, hi_)) for kc in range(8)])
                          res.append((b, lo_, hi_, n))
                      return res

                  for c in range(4):
                      for (b0, lo_, hi_, n) in inproj_fm(13 + c):
                          plo = lo_
                          if s == 0 and lo_ == 0:
                              S.op("act", lambda e: e.copy(xs[:, c, :, 3:7], P[b0][:, 0:NSMP].rearrange("p (b t) -> p b t", t=4)),
                                   reads=[pk(b0)], writes=["xs"])
                              plo = NSMP
                          S.op("act", lambda e: e.copy(F1[:, XB0 + c, plo + xb_off:hi_ + xb_off], P[b0][:, plo - lo_:n]),
                               reads=[pk(b0)], writes=akeys("F1", [XB0 + c], s, plo + xb_off, hi_ + xb_off))
                  for c in range(4):
                      for (b0, lo_, hi_, n) in inproj_fm(17 + c):
                          S.op("act", lambda e: e.activation(F1[:, c, lo_:hi_], P[b0][:, 0:n], AF.Gelu_apprx_tanh),
                               reads=[pk(b0)], writes=akeys("F1", [c], s, lo_, hi_))
                  if s == 0:
                      S.op("pool", lambda e: e.memset(F1[:, XB0:XB0 + 4, 0:3], 0.0),
                           writes=akeys("F1", [XB0 + c for c in range(4)], s, 0, 3))
                  else:
                      S.op("pool", lambda e: e.tensor_copy(F1[:, XB0:XB0 + 4, 0:3], xbc[:, l, :, :]), reads=["xbc"],
                           writes=akeys("F1", [XB0 + c for c in range(4)], s, 0, 3))

                  fence(ATT_KEYS, LRU_KEYS)
                  lru_pre(0, *lsteps[0])
                  lru_cast(0, *lsteps[0])
                  lru_pre(1, *lsteps[1])
                  lru_A(0, *lsteps[0])
                  nqk = 0
                  for (u0, u1, dchunk) in [(0, 4, 0), (1, 5, 1), (2, 6, 2), (3, 7, 3), (8, 10, KT0), (9, 11, KT0 + 1)]:
                      w0, wk0 = WS.get(plan_idx[("in", s, l, u0)])
                      w1, wk1 = WS.get(plan_idx[("in", s, l, u1)])
                      for (lo_, hi_) in tiles:
                          nqk += 1
                          lru_advance((len(lsteps) * nqk) // (6 * len(tiles)) - lstate["k"])
                          n = hi_ - lo_
                          b0 = mm_bank(); b1 = mm_bank()
                          mm_group(P[b0][:, 0:n], pk(b0),
                                   [(w0[:, kc, :], act[:, kc, lo_:hi_], [wk0] + akeys("act", [kc], s, lo_, hi_)) for kc in range(8)])
                          mm_group(P[b1][:, 0:n], pk(b1),
                                   [(w1[:, kc, :], act[:, kc, lo_:hi_], [wk1] + akeys("act", [kc], s, lo_, hi_)) for kc in range(8)])
                          j = rr["ev"]; rr["ev"] ^= 1
                          S.op("dve", lambda e: e.tensor_tensor(tmpa[j][:, 0:n], P[b1][:, 0:n], rope[:, 1, lo_:hi_], ALU.mult),
                               reads=[pk(b1), "rope"], writes=[("tmpa", j)])
                          S.op("dve", lambda e: e.tensor_tensor(tmpb[j][:, 0:n], P[b0][:, 0:n], rope[:, 0, lo_:hi_], ALU.mult),
                               reads=[pk(b0), "rope"], writes=[("tmpb", j)])
                          S.op("pool", lambda e: e.tensor_tensor(H[:, dchunk, lo_:hi_], tmpb[j][:, 0:n], tmpa[j][:, 0:n], ALU.add),
                               reads=[("tmpb", j), ("tmpa", j)], writes=akeys("H", [dchunk], s, lo_, hi_))
                          if dchunk >= KT0:
                              g = dchunk - KT0
                              if s == 0 and lo_ == 0:
                                  S.op("pool", lambda e: e.tensor_tensor(kv32[:, g, 0:NSMP], tmpb[j][:, 0:NSMP], tmpa[j][:, 0:NSMP], ALU.add),
                                       reads=[("tmpb", j), ("tmpa", j)], writes=[("kv32", g, 0)])
                              if s == 1 and hi_ == TB:
                                  S.op("pool", lambda e: e.tensor_tensor(kv32[:, g, NSMP:NSMP + 128], tmpb[j][:, n - 128:n], tmpa[j][:, n - 128:n], ALU.add),
                                       reads=[("tmpb", j), ("tmpa", j)], writes=[("kv32", g, 1)])
                  w, wk = WS.get(plan_idx[("in", s, l, 12)])
                  for sgi, (a_, b_) in enumerate(segs):
                      nt = b_ - a_
                      bk = mm_bank()
                      mm_group(P[bk][0:nt, 0:128], pk(bk),
                               [(act[:, kc, a_:b_], w[:, kc, :], [wk] + akeys("act", [kc], s, a_, b_)) for kc in range(8)])
                      src = P[bk][0:nt, 0:128].rearrange("p (g d) -> p g d", g=2).unsqueeze(2).to_broadcast([nt, 2, 2, 64])
                      S.op("dve", lambda e: e.tensor_copy(vsb[0:nt, sgi, :, :].rearrange("p g (r d) -> p g r d", r=2), src),
                           reads=[pk(bk)], writes=[("vsb", sgi)] + [("H", c, sg) for c in (6, 7, 8) for sg in range(len(segs))])
                      want = (s == 0 and sgi == 0) or (s == 1 and sgi == len(segs) - 1)
                      if want:
                          slot = 0 if s == 0 else 1
                          S.op("act", lambda e: e.copy(v32[0:nt, slot, :], P[bk][0:nt, 0:128]), reads=[pk(bk)], writes=[("v32", slot)])
                          row0 = 0 if s == 0 else NSMP
                          S.dma("sp", vout[l, row0:row0 + nt, :], v32[0:nt, slot, :], reads=[("v32", slot)])
                  lru_advance(len(lsteps))
                  if s == 0:
                      for g in range(2):
                          S.dma("sp", koutT[l, g, :, 0:NSMP], kv32[:, g, 0:NSMP], reads=[("kv32", g, 0)])
                  else:
                      for g in range(2):
                          S.dma("sp", koutT[l, g, :, NSMP:NSMP + 128], kv32[:, g, NSMP:NSMP + 128], reads=[("kv32", g, 1)])

                  chk("A1_%d_%d" % (s, l))
                  def attn_p1(ui, g, q_lo, nq, kblocks, dst_oT):
                      pt = pT[ui % 3]
                      sb0 = 0
                      nkb = len(kblocks)
                      qk = akeys("H", [2 * g, 2 * g + 1], s, q_lo, q_lo + nq)
                      for half in range(2):
                          ps = slice(64 * half, 64 * half + 64)
                          for kb, (kap, kk, vap, vk, mk, nk) in enumerate(kblocks):
                              outp = P[sb0 + half][0:nk, kb * 256:(kb + 1) * 256].rearrange("p (c q) -> p c q", c=2)[:, :, 0:nq]
                              S.op("pe", lambda e, kap=kap, outp=outp: e.matmul(outp, lhsT=kap[ps, :], rhs=qT[ps, 2 * g:2 * g + 2, q_lo:q_lo + nq], start=True, stop=True),
                                   reads=kk + qk, writes=[pk(sb0 + half)], inc=(kb == nkb - 1))
                      for half in range(2):
                          same = all(kbl[5] == kblocks[0][5] for kbl in kblocks)
                          if same and nkb == 2 and nq == 128:
                              nk = kblocks[0][5]
                              S.op("act", lambda e, nk=nk: e.activation(pt[0:nk, :, half, :, :].rearrange("p k c q -> p k (c q)"), P[sb0 + half][0:nk, :].rearrange("p (k x) -> p k x", k=2), AF.Exp, scale=0.125),
                                   reads=[pk(sb0 + half)], writes=[("pT", ui % 3, half)])
                          else:
                              for kb, (kap, kk, vap, vk, mk, nk) in enumerate(kblocks):
                                  inp = P[sb0 + half][0:nk, kb * 256:(kb + 1) * 256].rearrange("p (c q) -> p c q", c=2)[:, :, 0:nq]
                                  S.op("act", lambda e, inp=inp, kb=kb, nk=nk: e.activation(pt[0:nk, kb, half, :, 0:nq], inp, AF.Exp, scale=0.125),
                                       reads=[pk(sb0 + half)], writes=[("pT", ui % 3, half)])
                  def attn_p1m(ui, g, q_lo, nq, kblocks, dst_oT):
                      pt = pT[ui % 3]
                      nkb = len(kblocks)
                      same = all(kbl[5] == kblocks[0][5] for kbl in kblocks)
                      for half in range(2):
                          if same and nkb == 2 and nq == 128 and kblocks[0][5] == 128:
                              S.op("dve", lambda e: e.tensor_tensor(pt[:, :, half, :, :], pt[:, :, half, :, :],
                                                                    msk[:, 0:2, :].unsqueeze(2).to_broadcast([128, 2, 2, 128]), ALU.mult),
                                   reads=[("pT", ui % 3, half), "msk"], writes=[("pT", ui % 3, half)])
                          else:
                              for kb, (kap, kk, vap, vk, mk, nk) in enumerate(kblocks):
                                  S.op("dve", lambda e, kb=kb, nk=nk, mk=mk: e.tensor_tensor(pt[0:nk, kb, half, :, 0:nq], pt[0:nk, kb, half, :, 0:nq],
                                                                                         mk.unsqueeze(1).to_broadcast([nk, 2, nq]), ALU.mult),
                                       reads=[("pT", ui % 3, half), "msk"], writes=[("pT", ui % 3, half)])

                  def attn_p2(ui, g, q_lo, nq, kblocks, dst_oT):
                      pt = pT[ui % 3]
                      dn = den[ui % 2]
                      nkb = len(kblocks)
                      pvb, smb = (2, 3) if ui % 2 == 0 else (4, 5)
                      for which, bank in (("pv", pvb), ("sum", smb)):
                          outp = P[bank][:, :].rearrange("p (h c q) -> p h c q", h=2, c=2)[:, :, :, 0:nq]
                          for kb, (kap, kk, vap, vk, mk, nk) in enumerate(kblocks):
                              lt = vap if which == "pv" else ones[0:nk, :]
                              last = (kb == nkb - 1) and which == "pv"
                              S.op("pe", lambda e, lt=lt, outp=outp, kb=kb, nk=nk, last=last: e.matmul(outp, lhsT=lt, rhs=pt[0:nk, kb, :, :, 0:nq], start=(kb == 0), stop=last),
                                   reads=[("pT", ui % 3, 0), ("pT", ui % 3, 1)] + (vk if which == "pv" else ["ones"]), writes=[pk(bank)], inc=last)
                          if which == "sum":
                              er = esr[0:1, g, :].rearrange("p (h c q) -> p h c q", h=2, c=2)[:, :, :, 0:nq]
                              S.op("pe", lambda e, outp=outp, er=er: e.matmul(outp, lhsT=ones[0:1, :], rhs=er, start=False, stop=True),
                                   reads=["ones", "esr"], writes=[pk(bank)], inc=True)
                      dv = dn[:, :].rearrange("p (h c q) -> p h c q", h=2, c=2)[:, :, :, 0:nq]
                      sv = P[smb][:, :].rearrange("p (h c q) -> p h c q", h=2, c=2)[:, :, :, 0:nq]
                      S.op("act", lambda e: e.activation(dv, sv, AF.Ln), reads=[pk(smb)], writes=[("den", ui % 2)])
                      S.op("act", lambda e: e.activation(dv, dv, AF.Exp, scale=-1.0), reads=[("den", ui % 2)], writes=[("den", ui % 2)])
                      for half in range(2):
                          ps = slice(64 * half, 64 * half + 64)
                          pvv = P[pvb][:, :].rearrange("p (h c q) -> p h c q", h=2, c=2)
                          dvv = dn[:, :].rearrange("p (h c q) -> p h c q", h=2, c=2)
                          S.op("dve", lambda e, ps=ps, half=half: e.tensor_tensor(dst_oT[ps, 2 * g:2 * g + 2, 0:nq], pvv[ps, half, :, 0:nq], dvv[ps, half, :, 0:nq], ALU.mult),
                               reads=[pk(pvb), ("den", ui % 2)], writes=[("oT", id(dst_oT), g)])

                  def attn_norm(oTt, lo_, nq):
                      i = rr["n"]; rr["n"] = (i + 1) % 3
                      rs = rstd[i]
                      okeys = [("oT", id(oTt), 0), ("oT", id(oTt), 1)]
                      S.op("pool", lambda e: e.tensor_tensor(sqa[:, :, 0:nq], oTt[:, :, 0:nq], oTt[:, :, 0:nq], ALU.mult), reads=okeys, writes=["sqa"])
                      for c in range(4):
                          S.op("pe", lambda e, c=c: e.matmul(P[6][:, 0:nq], lhsT=ones[:], rhs=sqa[:, c, 0:nq], start=(c == 0), stop=(c == 3)),
                               reads=["sqa", "ones"], writes=[pk(6)], inc=(c == 3))
                      S.op("act", lambda e: e.activation(rs[:, 0:nq], P[6][:, 0:nq], AF.Ln, bias=epsc[:, 1:2]),
                           reads=[pk(6), "epsc"], writes=[("rstd", i)])
                      S.op("act", lambda e: e.activation(rs[:, 0:nq], rs[:, 0:nq], AF.Exp, scale=-0.5), reads=[("rstd", i)], writes=[("rstd", i)])
                      for c in range(4):
                          S.op("dve", lambda e, c=c: e.scalar_tensor_tensor(act[:, c, lo_:lo_ + nq], oTt[:, c, 0:nq], prm[:, l, 32 + c:33 + c], rs[:, 0:nq], ALU.mult, ALU.mult),
                               reads=okeys + [("rstd", i), "prm"], writes=akeys("act", [c], s, lo_, lo_ + nq))

                  def sample_attention():
                      oTt = oT[0]
                      es = esink[:, l, 0:8].unsqueeze(2).to_broadcast([128, 8, 64])
                      qk = akeys("H", [0, 1, 2, 3], s, 0, NSMP)
                      kk_new = akeys("H", [KT0, KT0 + 1], s, 0, NSMP)
                      for half in range(2):
                          ps = slice(64 * half, 64 * half + 64)
                          for g in range(2):
                              outp = P[half][0:NSMP, 256 + g * 128:256 + (g + 1) * 128].rearrange("p (c q) -> p c q", c=2)
                              S.op("pe", lambda e, outp=outp, g=g, ps=ps: e.matmul(outp, lhsT=H[ps, KT0 + g, 0:NSMP], rhs=qT[ps, 2 * g:2 * g + 2, 0:NSMP], start=True, stop=True),
                                   reads=kk_new + qk, writes=[pk(half), ("Pn", half)])
                      def load_caches(grp):
                          b0 = 4 * grp
                          kcA_, kcB_, vc_ = kcA[grp % 2], kcB[grp % 2], vc[grp % 2]
                          kAk, kBk, vck = ("kcA", 0), ("kcB", 0), ("vc", 0)
                          srck = cache_k[l, b0:b0 + 4].rearrange("b j x -> j b x")
                          srcv = cache_v[l, b0:b0 + 4].rearrange("b j (g d) -> j b g d", g=2)
                          S.dma("pool", kcA_, srck, writes=[kAk])
                          for g in range(2):
                              S.dma("pool", vc_[:, :, g, 0, :], srcv[:, :, g, :], writes=[vck])
                          S.op("dve", lambda e: e.tensor_copy(kcB_[:, :, 0:64], kcA_[:, :, 64:128]), reads=[kAk], writes=[kBk])
                          S.op("dve", lambda e: e.tensor_copy(kcB_[:, :, 64:128], kcA_[:, :, 0:64]), reads=[kAk, kBk], writes=[kBk])
                          S.op("dve", lambda e: e.tensor_copy(vc_[:, :, :, 1, :], vc_[:, :, :, 0, :]), reads=[vck], writes=[vck])
                      for grp in range(4):
                          load_caches(grp)
                          b0 = 4 * grp
                          kcA_, kcB_, kTc_, vc_ = kcA[grp % 2], kcB[grp % 2], kTc[grp % 2], vc[grp % 2]
                          kAk, kBk, kTk, vck = ("kcA", 0), ("kcB", 0), ("kTc", 0), ("vc", 0)
                          for bb in range(4):
                              for ab, src in ((0, kcA_), (1, kcB_)):
                                  S.op("pe", lambda e, ab=ab, src=src, bb=bb: e.transpose(PT[:, (bb * 2 + ab) * 128:(bb * 2 + ab + 1) * 128], src[:, bb, :], ident[:]),
                                       reads=[kAk, kBk, "ident"], writes=[pk(7)], inc=(bb == 3 and ab == 1))
                          S.op("act", lambda e: e.copy(kTc_.rearrange("p b a j -> p (b a j)"), PT[:, :]),
                               reads=[pk(7)], writes=[kTk])
                          for half in range(2):
                              ps = slice(64 * half, 64 * half + 64)
                              for b in range(4):
                                  for g in range(2):
                                      ab = 0 if g == half else 1
                                      bg = b0 + b
                                      outp = P[half][:, g * 128:(g + 1) * 128].rearrange("p (c b t) -> p c b t", c=2, b=NSEQ)[:, :, bg, :]
                                      last = (b == 3 and g == 1)
                                      S.op("pe", lambda e, outp=outp, b=b, g=g, ab=ab, ps=ps, bg=bg: e.matmul(outp, lhsT=kTc_[ps, b, ab, :], rhs=qT[ps, 2 * g:2 * g + 2, 4 * bg:4 * bg + 4], start=True, stop=True),
                                           reads=[kTk] + qk, writes=[pk(half), ("Pc", half, grp)], inc=last)
                          for half in range(2):
                              inp = P[half][:, 0:256].rearrange("p (x b t) -> p x b t", x=4, b=NSEQ)[:, :, b0:b0 + 4, :]
                              outp = pTc[:, half * 256:(half + 1) * 256].rearrange("p (x b t) -> p x b t", x=4, b=NSEQ)[:, :, b0:b0 + 4, :]
                              S.op("act", lambda e, inp=inp, outp=outp: e.activation(outp, inp, AF.Exp, scale=0.125),
                                   reads=[pk(half), ("Pc", half, grp)], writes=[("pTc", half, grp)])
                              S.op("dve", lambda e, outp=outp: e.tensor_tensor(outp, outp, msk[:, 3, 0:4].unsqueeze(1).unsqueeze(1).to_broadcast([128, 4, 4, 4]), ALU.mult),
                                   reads=[("pTc", half, grp), "msk"], writes=[("pTc", half, grp)])
                          for b in range(4):
                              for g in range(2):
                                  bg = b0 + b
                                  rh = pTc[:, :].rearrange("p (h g c b t) -> p h g c b t", h=2, g=2, c=2, b=NSEQ)[:, :, g, :, bg, :]
                                  outp = P[2][:, :].rearrange("p (h g c b t) -> p h g c b t", h=2, g=2, c=2, b=NSEQ)[:, :, g, :, bg, :]
                                  S.op("pe", lambda e, rh=rh, outp=outp, b=b, g=g: e.matmul(outp, lhsT=vc_[:, b, g, :, :].rearrange("p r d -> p (r d)"), rhs=rh, start=True, stop=True),
                                       reads=[vck, ("pTc", 0, grp), ("pTc", 1, grp)], writes=[pk(2)], inc=(b == 3 and g == 1))
                      for half in range(2):
                          S.op("act", lambda e, half=half: e.activation(pTn[0:NSMP, half * 256:(half + 1) * 256], P[half][0:NSMP, 256:512], AF.Exp, scale=0.125),
                               reads=[pk(half), ("Pn", half)], writes=[("pTn", half)])
                          o2 = pTn[0:NSMP, half * 256:(half + 1) * 256].rearrange("p (x q) -> p x q", x=4)
                          S.op("dve", lambda e, o2=o2: e.tensor_tensor(o2, o2, msk[0:NSMP, 4, 0:64].unsqueeze(1).to_broadcast([NSMP, 4, 64]), ALU.mult),
                               reads=[("pTn", half), "msk"], writes=[("pTn", half)])
                      for g in range(2):
                          rh = pTn[0:NSMP, :].rearrange("p (h g x) -> p h g x", h=2, g=2)[:, :, g, :]
                          outp = P[6][:, :].rearrange("p (h g x) -> p h g x", h=2, g=2)[:, :, g, :]
                          S.op("pe", lambda e, rh=rh, outp=outp, g=g: e.matmul(outp, lhsT=vsb[0:NSMP, 0, g, :], rhs=rh, start=True, stop=True),
                               reads=[("vsb", 0), ("pTn", 0), ("pTn", 1)], writes=[pk(6)], inc=(g == 1))
                      S.op("pe", lambda e: e.matmul(P[3][:, :], lhsT=ones[:], rhs=pTc[:, :], start=True, stop=False),
                           reads=["ones"] + [("pTc", h_, g_) for h_ in range(2) for g_ in range(4)], writes=[pk(3)], inc=False)
                      S.op("pe", lambda e: e.matmul(P[3][:, :], lhsT=ones[0:NSMP, :], rhs=pTn[0:NSMP, :], start=False, stop=True),
                           reads=["ones", ("pTn", 0), ("pTn", 1)], writes=[pk(3)])
                      dn = den[0]
                      S.op("dve", lambda e: e.tensor_tensor(dn[:, :].rearrange("p (x q) -> p x q", x=8), P[3][:, :].rearrange("p (x q) -> p x q", x=8), es, ALU.add),
                           reads=[pk(3), "esink"], writes=[("den", 0)])
                      S.op("act", lambda e: e.activation(dn[:, :], dn[:, :], AF.Ln), reads=[("den", 0)], writes=[("den", 0)])
                      S.op("act", lambda e: e.activation(dn[:, :], dn[:, :], AF.Exp, scale=-1.0), reads=[("den", 0)], writes=[("den", 0)])
                      S.op("dve", lambda e: e.tensor_copy(den[1][:, :], P[2][:, :]), reads=[pk(2)], writes=[("den", 1)])
                      S.op("dve", lambda e: e.tensor_tensor(den[1][:, :], den[1][:, :], P[6][:, :], ALU.add), reads=[pk(6), ("den", 1)], writes=[("den", 1)])
                      for half in range(2):
                          ps = slice(64 * half, 64 * half + 64)
                          nv = den[1][:, :].rearrange("p (h x q) -> p h x q", h=2, x=4)
                          dv = dn[:, :].rearrange("p (h x q) -> p h x q", h=2, x=4)
                          S.op("dve", lambda e, ps=ps, half=half: e.tensor_tensor(oTt[ps, :, 0:64], nv[ps, half, :, :], dv[ps, half, :, :], ALU.mult),
                               reads=[("den", 0), ("den", 1)], writes=[("oT", id(oTt), 0), ("oT", id(oTt), 1)])
                      attn_norm(oTt, 0, NSMP)

                  fence(LRU_KEYS, ATT_KEYS)
                  S.op("dve", lambda e: e.tensor_copy(esr.rearrange("p g (x q) -> p g x q", x=4),
                                                      esink[0:1, l, 8:16].rearrange("p (g x) -> p g x", g=2).unsqueeze(3).to_broadcast([1, 2, 4, 128])),
                       reads=["esink"], writes=["esr"])
                  for (lo_, hi_) in tiles:
                      lru_norm(lo_, hi_)
                  units = []
                  if s == 0:
                      sample_attention()
                      for g in range(2):
                          kbm = [(H[:, KT0 + g, 64:80], akeys("H", [KT0 + g], s, 64, 80), vsb[0:16, 1, g, :], [("vsb", 1)], msk[0:16, 1, 0:16], 16)]
                          units.append((g, 64, 16, kbm, oT[1], g == 1))
                  blocks = list(range(2, 10)) if s == 0 else list(range(0, 8))
                  for bi, sgi in enumerate(blocks):
                      a_, b_ = segs[sgi]
                      oTt = oT[(bi + 2) % 3]
                      for g in range(2):
                          cur = (H[:, KT0 + g, a_:b_], akeys("H", [KT0 + g], s, a_, b_), vsb[:, sgi, g, :], [("vsb", sgi)], msk[:, 1, :], 128)
                          if s == 0 and sgi == 2:
                              prev = (H[:, KT0 + g, 64:80], akeys("H", [KT0 + g], s, 64, 80), vsb[0:16, 1, g, :], [("vsb", 1)], msk[0:16, 2, :], 16)
                          elif s == 1 and sgi == 0:
                              prev = (kT_c[:, l, g, :], ["kT_c"], v_c[:, l, g, :], ["v_c"], msk[:, 0, :], 128)
                          else:
                              pa, pb = segs[sgi - 1]
                              prev = (H[:, KT0 + g, pa:pb], akeys("H", [KT0 + g], s, pa, pb), vsb[:, sgi - 1, g, :], [("vsb", sgi - 1)], msk[:, 0, :], 128)
                          units.append((g, a_, 128, [prev, cur], oTt, g == 1))
                  nun = len(units)
                  pend = []
                  for j0 in range(min(2, nun)):
                      attn_p1(j0, *units[j0][:5])
                      attn_p1m(j0, *units[j0][:5])
                  for i in range(nun):
                      if i + 2 < nun:
                          attn_p1(i + 2, *units[i + 2][:5])
                          attn_p1m(i + 2, *units[i + 2][:5])
                      attn_p2(i, *units[i][:5])
                      for _f in range(NFILL):
                          S.op("pe", lambda e: e.matmul(P[7][:, :], lhsT=ones[:], rhs=msk[:, 0:4, :].rearrange("p a q -> p (a q)"), start=True, stop=True),
                               reads=["ones", "msk"], writes=[pk(7)], inc=False)
                      if pend and pend[0][3] <= i - 3:
                          attn_norm(*pend.pop(0)[:3])
                      if units[i][5]:
                          pend.append((units[i][4], units[i][1], units[i][2], i))
                  for pn in pend:
                      attn_norm(*pn[:3])

                  if s == 0:
                      la, lb = segs[-1]
                      S.op("pool", lambda e: e.tensor_copy(kT_c[:, l, :, :], H[:, KT0:KT0 + 2, la:lb]), reads=akeys("H", [KT0, KT0 + 1], s, la, lb), writes=["kT_c"])
                      S.op("pool", lambda e: e.tensor_copy(v_c[:, l, :, :], vsb[:, 9, :, :]), reads=[("vsb", 9)], writes=["v_c"])
                      for c in range(4):
                          S.op("pool", lambda e, c=c: e.tensor_copy(h_c[:, l, c:c + 1], hstate[c]), reads=["hlast%d" % c], writes=["h_c"])
                      S.op("pool", lambda e: e.tensor_copy(cfin[:, :, :].rearrange("p c (b j) -> p c b j", j=3), xs[:, :, :, 4:7]),
                           reads=["xs"], writes=["cfin_s"])
                      S.dma("sp", cout_s[l], cfin[:, :, :], reads=["cfin_s"])
                      S.dma("sp", hout_s[l], hfin[:, :, :], reads=["hfin"])
                      S.op("pool", lambda e: e.tensor_copy(xbc[:, l, :, :], F1[:, XB0:XB0 + 4, TA + xb_off - 3:TA + xb_off]),
                           reads=akeys("F1", [XB0 + c for c in range(4)], s, TA + xb_off - 3, TA + xb_off), writes=["xbc"])
                  else:
                      for c in range(4):
                          S.op("pool", lambda e, c=c: e.tensor_copy(hfinp[:, c:c + 1], hstate[c]), reads=["hlast%d" % c], writes=["hfin_p"])
                      S.dma("sp", hout_p[l], hfinp[:, :], reads=["hfin_p"])
                      S.op("pool", lambda e: e.tensor_copy(cfinp[:, :, :], F1[:, XB0:XB0 + 4, TB + xb_off - 3:TB + xb_off]),
                           reads=akeys("F1", [XB0 + c for c in range(4)], s, TB + xb_off - 3, TB + xb_off), writes=["cfin_p"])
                      S.dma("sp", cout_p[l], cfinp[:, :, :], reads=["cfin_p"])

                  chk("A2_%d_%d" % (s, l))
                  def mm_out(m, wwk, lo_, hi_):
                      w, wk = wwk
                      n = hi_ - lo_
                      b = mm_bank()
                      mm_group(P[b][:, 0:n], pk(b),
                               [(w[:, kc, :], act[:, kc, lo_:hi_], [wk] + akeys("act", [kc], s, lo_, hi_)) for kc in range(8)])
                      if m % 2 == 0:
                          S.op("act", lambda e: e.copy(F1[:, m, lo_:hi_], P[b][:, 0:n]), reads=[pk(b)], writes=akeys("F1", [m], s, lo_, hi_))
                      else:
                          S.op("dve", lambda e: e.tensor_copy(F1[:, m, lo_:hi_], P[b][:, 0:n]), reads=[pk(b)], writes=akeys("F1", [m], s, lo_, hi_))
                  tile_outer_phase([plan_idx[("out", s, l, m)] for m in range(8)], tiles, mm_out,
                                   lambda lo_, hi_: postnorm_residual(l, s, lo_, hi_, 8),
                                   lambda lo_, hi_: prenorm(l, s, lo_, hi_, 16))

                  chk("A3_%d_%d" % (s, l))
                  for hf in range(2):
                      for c in range(11):
                          wg, wgk = WS.get(plan_idx[("g", s, l, hf, c)])
                          wu, wuk = WS.get(plan_idx[("u", s, l, hf, c)])
                          for (lo_, hi_) in tiles:
                              n = hi_ - lo_
                              bg_ = mm_bank(); bu_ = mm_bank()
                              mm_group(P[bg_][:, 0:n], pk(bg_),
                                       [(wg[:, kc, :], act[:, kc, lo_:hi_], [wgk] + akeys("act", [kc], s, lo_, hi_)) for kc in range(8)])
                              mm_group(P[bu_][:, 0:n], pk(bu_),
                                       [(wu[:, kc, :], act[:, kc, lo_:hi_], [wuk] + akeys("act", [kc], s, lo_, hi_)) for kc in range(8)])
                              j = rr["ev"]; rr["ev"] ^= 1
                              S.op("act", lambda e: e.activation(tmpb[j][:, 0:n], P[bg_][:, 0:n], AF.Silu), reads=[pk(bg_)], writes=[("tmpb", j)])
                              extra = [("vsb", sg) for sg in range(10)] if c in (6, 7, 8) else []
                              S.op("dve", lambda e: e.tensor_tensor(H[:, c, lo_:hi_], P[bu_][:, 0:n], tmpb[j][:, 0:n], ALU.mult),
                                   reads=[pk(bu_), ("tmpb", j)], writes=akeys("H", [c], s, lo_, hi_) + extra)
                      def mm_down(m, wwk, lo_, hi_, hf=hf):
                          w, wk = wwk
                          n = hi_ - lo_
                          b = mm_bank()
                          mm_group(P[b][:, 0:n], pk(b),
                                   [(w[:, kc, :], H[:, kc, lo_:hi_], [wk] + akeys("H", [kc], s, lo_, hi_)) for kc in range(11)])
                          fk = akeys("F1", [m], s, lo_, hi_)
                          if hf == 0:
                              S.op("act", lambda e: e.copy(F1[:, m, lo_:hi_], P[b][:, 0:n]), reads=[pk(b)], writes=fk)
                          else:
                              S.op("dve", lambda e: e.tensor_tensor(F1[:, m, lo_:hi_], P[b][:, 0:n], F1[:, m, lo_:hi_], ALU.add), reads=[pk(b)] + fk, writes=fk)
                      if hf == 0:
                          for m in range(8):
                              wwk = WS.get(plan_idx[("d", s, l, hf, m)])
                              for (lo_, hi_) in tiles:
                                  mm_down(m, wwk, lo_, hi_)
                      else:
                          tile_outer_phase([plan_idx[("d", s, l, 1, m)] for m in range(8)], tiles, mm_down,
                                           lambda lo_, hi_: postnorm_residual(l, s, lo_, hi_, 24),
                                           (lambda lo_, hi_: prenorm(l + 1, s, lo_, hi_, 0)) if l < DEPTH - 1 else None)

                  chk("B_%d_%d" % (s, l))
              for c in range(8):
                  if s == 0:
                      S.dma("sp", yT[c, :, 0:NSMP], x[:, c, 0:NSMP], reads=akeys("x", [c], s, 0, NSMP))
                      S.dma("sp", yT[c, :, NSMP:NSMP + 1024], x[:, c, 80:TA], reads=akeys("x", [c], s, 80, TA))
                  else:
                      S.dma("sp", yT[c, :, NSMP + 1024:NSMP + 2048], x[:, c, 0:TB], reads=akeys("x", [c], s, 0, TB))
        try:
            chk("setup")
            main_passes()
        except _Stop:
            pass
        S.finish("sp")
    return nc


_CACHE = {}


def _host_consts():
    half = 32
    inv = (np.float32(10000.0) ** (-np.arange(half, dtype=np.float32) / np.float32(half))).astype(np.float32)
    p = np.arange(128)
    d = p % 64
    fi = d % 32
    sign = np.where(d < 32, -1.0, 1.0).astype(np.float32)
    rope = np.zeros((2, 128, 2, TMAX), np.float32)
    posA = np.concatenate([np.tile(PAST + np.arange(4), NSEQ), np.arange(16), 16 + np.arange(1024)])
    posB = 16 + 1024 + np.arange(1024)
    for si, pos in enumerate([posA, posB]):
        ang = (pos.astype(np.float32)[None, :] * inv[fi][:, None]).astype(np.float32)
        a64 = ang.astype(np.float64)
        rope[si, :, 0, :len(pos)] = np.cos(a64).astype(np.float32)
        rope[si, :, 1, :len(pos)] = (np.sin(a64) * sign[:, None]).astype(np.float32)
    j = np.arange(128)[:, None]
    i = np.arange(128)[None, :]
    msk = np.zeros((128, 5, 128), np.float32)
    msk[:, 0, :] = (j > i)
    msk[:, 1, :] = (j <= i)
    msk[:, 2, :] = ((112 + j) > i) & (j < 16)
    msk[:, 3, :] = (j > i)
    bq, tq = np.arange(64) // 4, np.arange(64) % 4
    m4 = (bq[:, None] == bq[None, :]) & (tq[:, None] <= tq[None, :])
    msk[0:64, 4, 0:64] = m4
    ident = np.eye(128, dtype=np.float32)
    return rope, msk, ident


def _prep_weights(inp):
    f = lambda a: np.ascontiguousarray(a, dtype=np.float32)
    w_in = np.asarray(inp["w_in"])
    d = np.arange(64)
    rot = np.where(d < 32, d + 32, d - 32)
    cols = []
    for c in range(4):
        cols.append(c * 128 + np.arange(128))
    for c in range(4):
        cols.append(np.concatenate([c * 128 + hh * 64 + rot for hh in range(2)]))
    for g in range(2):
        cols.append(512 + 64 * g + np.concatenate([d, d]))
    for g in range(2):
        cols.append(512 + 64 * g + np.concatenate([rot, rot]))
    cols.append(640 + np.arange(128))
    for c in range(4):
        cols.append(768 + c * 128 + np.arange(128))
    for c in range(4):
        cols.append(1280 + c * 128 + np.arange(128))
    cols = np.stack(cols)
    wi = w_in[:, :, cols.reshape(-1)]
    w_in_u = f(wi.reshape(DEPTH, 8, 128, NU_IN, 128).transpose(0, 3, 2, 1, 4))
    w_out_u = f(np.asarray(inp["w_out"]).reshape(DEPTH, 8, 128, 8, 128).transpose(0, 3, 2, 1, 4))
    w_gate_u = f(np.asarray(inp["w_gate"]).reshape(DEPTH, 8, 128, NFF, 128).transpose(0, 3, 2, 1, 4))
    w_up_u = f(np.asarray(inp["w_up"]).reshape(DEPTH, 8, 128, NFF, 128).transpose(0, 3, 2, 1, 4))
    w_down_u = f(np.asarray(inp["w_down"]).reshape(DEPTH, 2, 11, 128, 8, 128).transpose(0, 1, 4, 3, 2, 5))
    w_lru = np.zeros((DEPTH, 128, 2, 4, 128), np.float32)
    for ai, nm in enumerate(["w_a", "w_i"]):
        w = np.asarray(inp[nm])
        for c in range(4):
            for bl in range(2):
                w_lru[:, bl * 64:(bl + 1) * 64, ai, c, bl * 64:(bl + 1) * 64] = w[:, 2 * c + bl]
    prm = np.zeros((128, DEPTH, 88), np.float32)

    def fm(v, nch):
        return np.asarray(v).reshape(DEPTH, nch, 128).transpose(2, 0, 1)
    prm[:, :, 0:8] = fm(inp["pre_mix_norm"], 8)
    prm[:, :, 8:16] = fm(inp["post_mix_norm"], 8)
    prm[:, :, 16:24] = fm(inp["pre_ffn_norm"], 8)
    prm[:, :, 24:32] = fm(inp["post_ffn_norm"], 8)
    prm[:, :, 32:36] = fm(inp["attn_out_norm"], 4)
    prm[:, :, 36:40] = fm(inp["lru_out_norm"], 4)
    cw = np.asarray(inp["conv_w"]).reshape(DEPTH, 4, 4, 128)
    prm[:, :, 40:56] = cw.transpose(3, 0, 2, 1).reshape(128, DEPTH, 16)
    prm[:, :, 56:60] = fm(inp["conv_b"], 4)
    prm[:, :, 60:64] = fm(inp["b_a"], 4)
    prm[:, :, 64:68] = fm(inp["b_i"], 4)
    prm[:, :, 68:72] = fm(inp["lam"], 4)
    sk = np.asarray(inp["sinks"])
    so = [4 * g + 2 * cp + half for half in range(2) for g in range(2) for cp in range(2)]
    po = [4 * g + 2 * cp + half for g in range(2) for half in range(2) for cp in range(2)]
    prm[:, :, 72:80] = sk[:, so][None]
    prm[:, :, 80:88] = sk[:, po][None]
    return dict(w_in_u=w_in_u, w_out_u=w_out_u, w_gate_u=w_gate_u, w_up_u=w_up_u,
                w_down_u=w_down_u, w_lru=w_lru, prm=prm)


def kernel(**inputs):
    inp = {k: np.asarray(v) for k, v in inputs.items()}
    if "nc" not in _CACHE:
        _CACHE["nc"] = build_program()
    nc = _CACHE["nc"]
    rope, msk, ident = _host_consts()
    shared = _prep_weights(inp)
    shared.update(rope=rope, msk=msk, ident=ident)
    f = lambda a: np.ascontiguousarray(a, dtype=np.float32)
    in_maps = []
    for c in range(NCORES):
        sq = slice(NSEQ * c, NSEQ * (c + 1))
        xa = np.concatenate([inp["x_sample"][sq].reshape(NSMP, D), inp["meta_tokens"], inp["x_prompt"][c]], axis=0)
        m = dict(shared)
        m["xT_in"] = f(xa.T.reshape(8, 128, TA + TB))
        m["cache_k"] = f(inp["cache_k"][:, sq].reshape(DEPTH, NSEQ, 128, 128))
        m["cache_v"] = f(inp["cache_v"][:, sq].reshape(DEPTH, NSEQ, 128, 128))
        m["sth"] = f(inp["state_h"][:, sq].reshape(DEPTH, NSEQ, 4, 128).transpose(0, 3, 2, 1))
        m["stc"] = f(inp["state_conv"][:, sq].reshape(DEPTH, NSEQ, 3, 4, 128).transpose(0, 4, 3, 1, 2))
        in_maps.append(m)
    res = run_bass_kernel_spmd(nc, in_maps, core_ids=list(range(NCORES)))
    R = res.results
    y_prompt = np.zeros((8, SEQ, D), np.float32)
    y_sample = np.zeros((128, 4, D), np.float32)
    prompt_k = np.zeros((DEPTH, 8, 128, 2, 64), np.float32)
    prompt_v = np.zeros((DEPTH, 8, 128, 2, 64), np.float32)
    prompt_h = np.zeros((DEPTH, 8, 512), np.float32)
    prompt_conv = np.zeros((DEPTH, 8, 3, 512), np.float32)
    sample_k = np.zeros((DEPTH, 128, 128, 2, 64), np.float32)
    sample_v = np.zeros((DEPTH, 128, 128, 2, 64), np.float32)
    sample_h = np.zeros((DEPTH, 128, 512), np.float32)
    sample_conv = np.zeros((DEPTH, 128, 3, 512), np.float32)
    for c in range(NCORES):
        r = R[c]
        sq = slice(NSEQ * c, NSEQ * (c + 1))
        yT = np.asarray(r["yT"]).reshape(D, NSMP + SEQ)
        y_prompt[c] = yT[:, NSMP:].T
        y_sample[sq] = yT[:, :NSMP].T.reshape(NSEQ, 4, D)
        kT = np.asarray(r["koutT"])[:, :, 0:64, :]
        prompt_k[:, c] = kT[:, :, :, NSMP:].transpose(0, 3, 1, 2)
        vo = np.asarray(r["vout"])
        prompt_v[:, c] = vo[:, NSMP:].reshape(DEPTH, 128, 2, 64)
        prompt_h[:, c] = np.asarray(r["hout_p"]).transpose(0, 2, 1).reshape(DEPTH, 512)
        sample_h[:, sq] = np.asarray(r["hout_s"]).transpose(0, 3, 2, 1).reshape(DEPTH, NSEQ, 512)
        prompt_conv[:, c] = np.asarray(r["cout_p"]).transpose(0, 3, 2, 1).reshape(DEPTH, 3, 512)
        sample_conv[:, sq] = np.asarray(r["cout_s"]).reshape(DEPTH, 128, 4, NSEQ, 3).transpose(0, 3, 4, 2, 1).reshape(DEPTH, NSEQ, 3, 512)
        sample_k[:, sq, 0:124] = np.asarray(r["sk_old"]).reshape(DEPTH, NSEQ, 124, 2, 64)
        sample_v[:, sq, 0:124] = np.asarray(r["sv_old"]).reshape(DEPTH, NSEQ, 124, 2, 64)
        kn = kT[:, :, :, :NSMP].reshape(DEPTH, 2, 64, NSEQ, 4)
        sample_k[:, sq, 124:128] = kn.transpose(0, 3, 4, 1, 2)
        sample_v[:, sq, 124:128] = vo[:, :NSMP].reshape(DEPTH, NSEQ, 4, 2, 64)
    return (y_prompt, y_sample, prompt_k, prompt_v, prompt_h, prompt_conv,
            sample_k, sample_v, sample_h, sample_conv)
```

```python
import contextlib
import numpy as np
import concourse.bass as bass
import concourse.mybir as mybir
from concourse.bass_utils import run_bass_kernel_spmd

F32 = mybir.dt.float32
BF16 = mybir.dt.bfloat16
AF = mybir.ActivationFunctionType
ALU = mybir.AluOpType

NCORES = 8
DEPTH = 4
D = 1024
SEQ = 2048
NMETA = 16
NSMP = 64
NSEQ = 16
DFF = 2816
NFF = 22
EPS = 1e-6
PAST = 8192
TA = 64 + 16 + 1024
TB = 1024
TMAX = TA
NRING = 8
NFILL = 4
AHEAD = 5
NU_IN = 21
IN_ORDER = [13, 14, 15, 16, 17, 18, 19, 20, 0, 4, 1, 5, 2, 6, 3, 7, 8, 10, 9, 11, 12]


class Sched:
    def __init__(self, nc, stack, n_dma=10):
        self.nc = nc
        self.engs = {"pe": nc.tensor, "act": nc.scalar, "dve": nc.vector,
                     "pool": nc.gpsimd, "sp": nc.sync}
        self.sem, self.cnt, self.known = {}, {}, {}
        for n in self.engs:
            self.sem[n] = stack.enter_context(nc.semaphore("s_" + n))
            self.cnt[n] = 0
            self.known[n] = {}
        self.n_dma = n_dma
        for pre in ("d", "w"):
            for i in range(n_dma):
                n = "%s%d" % (pre, i)
                self.sem[n] = stack.enter_context(nc.semaphore("s_" + n))
                self.cnt[n] = 0
        self.buf = {}
        self.dma_rr = {"d": 0, "w": 0}

    def _need(self, eng, deps):
        e = self.engs[eng]
        kn = self.known[eng]
        for tl, v in deps.items():
            if v <= 0 or kn.get(tl, 0) >= v:
                continue
            if tl == eng:
                if eng == "pe" or eng == "sp" or v > self.cnt[eng]:
                    continue
            else:
                assert v <= self.cnt[tl], ("wait on un-issued event", eng, tl, v, self.cnt[tl])
            e.wait_ge(self.sem[tl], v)
            kn[tl] = v

    def _collect(self, reads, writes, eng=None):
        deps = {}
        for k in reads:
            st = self.buf.get(k)
            if st and st[0] is not None:
                tl, v = st[0]
                if deps.get(tl, 0) < v:
                    deps[tl] = v
            if st and isinstance(k, tuple) and k[0] == "P":
                for tl, v in st[1].items():
                    if tl != eng and deps.get(tl, 0) < v:
                        deps[tl] = v
        for k in writes:
            st = self.buf.get(k)
            if st:
                if st[0] is not None:
                    tl, v = st[0]
                    if deps.get(tl, 0) < v:
                        deps[tl] = v
                for tl, v in st[1].items():
                    if deps.get(tl, 0) < v:
                        deps[tl] = v
        return deps

    def _record(self, ev, reads, writes):
        tl, v = ev
        for k in reads:
            st = self.buf.get(k)
            if st is None:
                st = self.buf[k] = [None, {}]
            if st[1].get(tl, 0) < v:
                st[1][tl] = v
        for k in writes:
            self.buf[k] = [ev, {}]

    def op(self, eng, fn, reads=(), writes=(), inc=True):
        self._need(eng, self._collect(reads, writes, eng))
        ins = fn(self.engs[eng])
        ticket = self.cnt[eng] + 1
        if inc:
            ins.then_inc(self.sem[eng], 1)
            self.cnt[eng] = ticket
        self._record((eng, ticket), reads, writes)
        return ins

    def dma(self, q, out, in_, reads=(), writes=()):
        pre = "w" if q == "pool" else "d"
        ch = "%s%d" % (pre, self.dma_rr[pre])
        self.dma_rr[pre] = (self.dma_rr[pre] + 1) % self.n_dma
        deps = self._collect(reads, writes)
        if self.cnt[ch] > 0:
            deps[ch] = max(deps.get(ch, 0), self.cnt[ch])
        self._need(q, deps)
        ins = self.engs[q].dma_start(out=out, in_=in_)
        self.cnt[ch] += 16
        ins.then_inc(self.sem[ch], 16)
        self._record((ch, self.cnt[ch]), reads, writes)
        return ins

    def barrier(self):
        for eng in self.engs:
            self.finish(eng)

    def finish(self, eng="sp"):
        deps = {tl: c for tl, c in self.cnt.items() if c > 0 and tl != eng}
        self._need(eng, deps)


def seg_bounds(s):
    if s == 0:
        return [(0, 64), (64, 80)] + [(80 + 128 * n, 80 + 128 * (n + 1)) for n in range(8)]
    return [(128 * n, 128 * (n + 1)) for n in range(8)]


def tile_list(s):
    if s == 0:
        return [(0, 80), (80, 592), (592, 1104)]
    return [(0, 512), (512, 1024)]


def segs_of(s, lo, hi):
    return [i for i, (a, b) in enumerate(seg_bounds(s)) if a < hi and b > lo]


class _Stop(Exception):
    pass


DBG_STOP = None


def chk(name):
    if DBG_STOP is not None and DBG_STOP == name:
        raise _Stop()


def build_program():
    nc = bass.Bass("TRN2", target_bir_lowering=False)

    def din(name, shape, dt=F32):
        return nc.dram_tensor(name, list(shape), dt, kind="ExternalInput").ap()

    def dout(name, shape, dt=F32):
        return nc.dram_tensor(name, list(shape), dt, kind="ExternalOutput").ap()

    xT_in = din("xT_in", [8, 128, TA + TB])
    w_in_u = din("w_in_u", [DEPTH, NU_IN, 128, 8, 128])
    w_out_u = din("w_out_u", [DEPTH, 8, 128, 8, 128])
    w_gate_u = din("w_gate_u", [DEPTH, NFF, 128, 8, 128])
    w_up_u = din("w_up_u", [DEPTH, NFF, 128, 8, 128])
    w_down_u = din("w_down_u", [DEPTH, 2, 8, 128, 11, 128])
    w_lru = din("w_lru", [DEPTH, 128, 2, 4, 128])
    prm_in = din("prm", [128, DEPTH, 88])
    rope_in = din("rope", [2, 128, 2, TMAX])
    msk_in = din("msk", [128, 5, 128])
    ident_in = din("ident", [128, 128])
    cache_k = din("cache_k", [DEPTH, NSEQ, 128, 128])
    cache_v = din("cache_v", [DEPTH, NSEQ, 128, 128])
    sth_in = din("sth", [DEPTH, 128, 4, NSEQ])
    stc_in = din("stc", [DEPTH, 128, 4, NSEQ, 3])

    yT = dout("yT", [8, 128, NSMP + SEQ])
    koutT = dout("koutT", [DEPTH, 2, 128, NSMP + 128])
    vout = dout("vout", [DEPTH, NSMP + 128, 128])
    hout_p = dout("hout_p", [DEPTH, 128, 4])
    hout_s = dout("hout_s", [DEPTH, 128, 4, NSEQ])
    cout_p = dout("cout_p", [DEPTH, 128, 4, 3])
    cout_s = dout("cout_s", [DEPTH, 128, 4, 3 * NSEQ])
    sk_old = dout("sk_old", [DEPTH, NSEQ, 124, 128])
    sv_old = dout("sv_old", [DEPTH, NSEQ, 124, 128])

    with contextlib.ExitStack() as st:
        S = Sched(nc, st)

        def sb(name, shape, dt=F32):
            return st.enter_context(nc.sbuf_tensor(name, list(shape), dt))

        x = sb("x", [128, 8, TMAX])
        act = sb("act", [128, 8, TMAX], BF16)
        F1 = sb("F1", [128, 8, TMAX])
        H = sb("H", [128, 11, TMAX], BF16)
        gate = F1
        qT = H
        KT0 = 4
        vsb_flat = H[:, 6:9, :].rearrange("p a b -> p (a b)")[:, 0:10 * 256]
        vsb = vsb_flat.rearrange("p (n g d) -> p n g d", n=10, g=2)
        wsl = [sb("wsl%d" % i, [128, 11 * 128], BF16) for i in range(NRING)]
        rope = sb("rope_sb", [128, 2, TMAX])
        prm = sb("prm_sb", [128, DEPTH, 88])
        c8 = sb("c8", [128, DEPTH, 24])
        q25 = sb("q25", [128, 1])
        esink = sb("esink", [128, DEPTH, 16])
        msk = sb("msk_sb", [128, 5, 128], BF16)
        ident = sb("ident_sb", [128, 128], BF16)
        ones = sb("ones", [128, 128], BF16)
        one1 = sb("one1", [128, 1])
        epsc = sb("epsc", [128, 2])
        wl = [sb("wl0", [128, 2, 4, 128], BF16)] * 2
        kT_c = sb("kT_c", [128, DEPTH, 2, 128], BF16)
        v_c = sb("v_c", [128, DEPTH, 2, 128], BF16)
        h_c = sb("h_c", [128, DEPTH, 4])
        xbc = sb("xbc", [128, DEPTH, 4, 3])
        sq = [sb("sq%d" % i, [128, 512], BF16) for i in range(4)]
        rstd = [sb("rstd%d" % i, [128, 512]) for i in range(3)]
        tmpa = [sb("tmpa%d" % i, [128, 512]) for i in range(2)]
        tmpb = [sb("tmpb%d" % i, [128, 512]) for i in range(2)]
        kv32 = sb("kv32", [128, 2, NSMP + 128])
        v32 = sb("v32", [128, 2, 128])
        arena = sb("arena", [128, 6912])

        def av(off, words, dt=F32, p0=0, p1=128):
            a = arena[p0:p1, off:off + words]
            return a if dt == F32 else a.bitcast(BF16)
        pT = [av(o_, 512, BF16).rearrange("p (k h c q) -> p k h c q", h=2, k=2, c=2) for o_ in (0, 512, 5376)]
        den = [av(1024 + 512 * i, 512) for i in range(2)]
        oT = [av(o_, 512).rearrange("p (c q) -> p c q", c=4) for o_ in (2048, 2560, 5888)]
        sqa = av(3072, 256, BF16).rearrange("p (c q) -> p c q", c=4)
        pTc = av(3328, 256, BF16)
        pTn = av(3584, 256, BF16, 0, 64)
        kcA = [av(3840, 256, BF16).rearrange("p (b x) -> p b x", b=4)] * 2
        kcB = [av(3840 + 256, 256, BF16).rearrange("p (b x) -> p b x", b=4)] * 2
        kTc = [av(3840 + 512, 512, BF16).rearrange("p (b a j) -> p b a j", b=4, a=2)] * 2
        vc = [av(3840 + 1024, 512, BF16).rearrange("p (b g r d) -> p b g r d", b=4, g=2, r=2)] * 2
        lt = [[av(1536 * i + 512 * j, 512) for j in range(3)] for i in range(2)]
        u32 = [av(3072 + 512 * i, 512) for i in range(3)]
        u16 = [av(4608 + 256 * i, 256, BF16) for i in range(3)]
        fsc = sb("fsc", [128, 2])
        esr = av(6400, 512, BF16, 0, 1).rearrange("p (g x) -> p g x", g=2)
        xs = sb("xs", [128, 4, NSEQ, 7])
        h0s = sb("h0s", [128, 4, NSEQ])
        ATT_KEYS = ([("pT", i, h) for i in range(3) for h in range(2)] + [("den", i) for i in range(2)]
                    + [("oT", id(oT[i]), g) for i in range(3) for g in range(2)] + ["sqa"]
                    + [(nm, i) for nm in ("kcA", "kcB", "kTc", "vc") for i in range(2)]
                    + [("pTc", h, g) for h in range(2) for g in range(4)] + [("pTn", h) for h in range(2)] + ["esr"])
        LRU_KEYS = ([(nm, i) for nm in ("lt1", "lt2", "lt3") for i in range(2)] + [(nm, i) for nm in ("u32", "u16") for i in range(3)])

        def fence(wait_keys, set_keys):
            S.op("pool", lambda e: e.memset(fsc[:, 0:1], 0.0), writes=list(wait_keys) + list(set_keys))

        P = [st.enter_context(nc.psum_tensor("P%d" % i, [128, 512], F32)) for i in range(8)]
        PT = P[7][:].bitcast(BF16)

        def pk(i):
            return ("P", i)

        S.dma("sp", prm[:], prm_in, writes=["prm"])
        S.dma("pool", msk[:], msk_in, writes=["msk"])
        S.dma("pool", ident[:], ident_in, writes=["ident"])
        S.op("pool", lambda e: e.memset(ones[:], 1.0), writes=["ones"])
        S.op("pool", lambda e: e.memset(one1[:], 1.0), writes=["one1"])
        S.op("pool", lambda e: e.memset(epsc[:, 0:1], 1024.0 * EPS), writes=["epsc"])
        S.op("pool", lambda e: e.memset(epsc[:, 1:2], 512.0 * EPS), writes=["epsc"])
        S.op("dve", lambda e: e.tensor_scalar(prm[:, :, 0:32], prm[:, :, 0:32], float(np.sqrt(1024.0)), None, ALU.mult),
             reads=["prm"], writes=["prm"])
        S.op("dve", lambda e: e.tensor_scalar(prm[:, :, 32:40], prm[:, :, 32:40], float(np.sqrt(512.0)), None, ALU.mult),
             reads=["prm"], writes=["prm"])
        S.op("act", lambda e: e.activation(c8[:, :, 0:4], prm[:, :, 68:72], AF.Exp, scale=-1.0),
             reads=["prm"], writes=["c8"])
        S.op("act", lambda e: e.activation(c8[:, :, 0:4], c8[:, :, 0:4], AF.Ln, bias=one1[:]),
             reads=["c8", "one1"], writes=["c8"])
        S.op("dve", lambda e: e.tensor_scalar(c8[:, :, 4:8], c8[:, :, 0:4], -16.0, None, ALU.mult),
             reads=["c8"], writes=["c8b"])
        S.op("dve", lambda e: e.tensor_scalar(c8[:, :, 0:4], c8[:, :, 0:4], -8.0, None, ALU.mult),
             reads=["c8", "c8b"], writes=["c8"])
        S.op("dve", lambda e: e.tensor_scalar(c8[:, :, 8:12], c8[:, :, 0:4], 0.5, None, ALU.mult), reads=["c8"], writes=["c8c"])
        S.op("dve", lambda e: e.tensor_scalar(c8[:, :, 16:24], prm[:, :, 60:68], 0.5, None, ALU.mult), reads=["prm", "c8", "c8b", "c8c"], writes=["c8"])
        S.op("pool", lambda e: e.memset(q25[:], 0.25 + 5e-7), writes=["q25"])
        S.op("act", lambda e: e.activation(esink[:], prm[:, :, 72:88], AF.Exp), reads=["prm"], writes=["esink"])
        CONST = ["prm", "c8", "esink", "msk", "ident", "ones", "one1", "q25"]

        wstate = {"n": 0}

        def load_unit(src_ap, nk):
            i = wstate["n"]
            wstate["n"] += 1
            ws = wsl[i % NRING]
            S.dma("pool", ws[:, 0:nk * 128], src_ap.rearrange("p k n -> p (k n)"), writes=[("wsl", i % NRING)])
            return ws[:, 0:nk * 128].rearrange("p (k n) -> p k n", k=nk), ("wsl", i % NRING)

        class WStream:
            def __init__(self):
                self.plan = []
                self.loaded = []
                self.next = 0

            def add(self, src_ap, nk):
                self.plan.append((src_ap, nk))
                return len(self.plan) - 1

            def get(self, idx, ahead=AHEAD):
                while self.next <= min(max(idx + ahead, idx), len(self.plan) - 1):
                    self.loaded.append(load_unit(*self.plan[self.next]))
                    self.next += 1
                return self.loaded[idx]

        WS = WStream()
        plan_idx = {}
        for s in range(2):
            for l in range(DEPTH):
                for uidx in IN_ORDER:
                    plan_idx[("in", s, l, uidx)] = WS.add(w_in_u[l, uidx], 8)
                for m in range(8):
                    plan_idx[("out", s, l, m)] = WS.add(w_out_u[l, m], 8)
                for hf in range(2):
                    for c in range(11):
                        plan_idx[("g", s, l, hf, c)] = WS.add(w_gate_u[l, 11 * hf + c], 8)
                        plan_idx[("u", s, l, hf, c)] = WS.add(w_up_u[l, 11 * hf + c], 8)
                    for m in range(8):
                        plan_idx[("d", s, l, hf, m)] = WS.add(w_down_u[l, hf, m], 11)

        rr = {"mm": 0, "n": 0, "ev": 0, "sq": 0}

        def mm_bank():
            i = rr["mm"]
            rr["mm"] = (i + 1) % 4
            return i

        def mm_group(out_ap, out_key, terms):
            n = len(terms)
            for j, (lt, rh, rk) in enumerate(terms):
                S.op("pe", lambda e, lt=lt, rh=rh, j=j: e.matmul(out_ap, lhsT=lt, rhs=rh, start=(j == 0), stop=(j == n - 1)),
                     reads=rk, writes=[out_key], inc=(j == n - 1))

        def akeys(buf, chunks, s, lo_, hi_):
            return [(buf, c, sg) for c in chunks for sg in segs_of(s, lo_, hi_)]

        def rms_apply(s, lo_, hi_, src, src_name, nch, ch0, gcol, l, dst_fn, extra_reads=(), sq_eng="act"):
            n = hi_ - lo_
            i = rr["n"]; rr["n"] = (i + 1) % 3
            rs = rstd[i]
            bank = 6
            for c in range(nch):
                qi = rr["sq"]; rr["sq"] = (qi + 1) % 4
                sqt = sq[qi]
                if sq_eng == "act":
                    S.op("act", lambda e, c=c, sqt=sqt: e.activation(sqt[:, 0:n], src[:, ch0 + c, lo_:hi_], AF.Square),
                         reads=akeys(src_name, [ch0 + c], s, lo_, hi_) + list(extra_reads), writes=[("sq", qi)])
                else:
                    S.op("pool", lambda e, c=c, sqt=sqt: e.tensor_tensor(sqt[:, 0:n], src[:, ch0 + c, lo_:hi_], src[:, ch0 + c, lo_:hi_], ALU.mult),
                         reads=akeys(src_name, [ch0 + c], s, lo_, hi_) + list(extra_reads), writes=[("sq", qi)])
                S.op("pe", lambda e, c=c, sqt=sqt: e.matmul(P[bank][:, 0:n], lhsT=ones[:], rhs=sqt[:, 0:n], start=(c == 0), stop=(c == nch - 1)),
                     reads=[("sq", qi), "ones"], writes=[pk(bank)], inc=True)
            dd = 1024.0 if nch == 8 else 512.0
            S.op("act", lambda e: e.activation(rs[:, 0:n], P[bank][:, 0:n], AF.Ln, bias=(epsc[:, 0:1] if nch == 8 else epsc[:, 1:2])),
                 reads=[pk(bank), "epsc"], writes=[("rstd", i)])
            S.op("act", lambda e: e.activation(rs[:, 0:n], rs[:, 0:n], AF.Exp, scale=-0.5), reads=[("rstd", i)], writes=[("rstd", i)])
            for c in range(nch):
                dst_fn(c, rs[:, 0:n], ("rstd", i))

        def prenorm(l, s, lo_, hi_, gcol):
            def app(c, rs, rk):
                S.op("dve", lambda e: e.scalar_tensor_tensor(act[:, c, lo_:hi_], x[:, c, lo_:hi_], prm[:, l, gcol + c:gcol + c + 1], rs, ALU.mult, ALU.mult),
                     reads=akeys("x", [c], s, lo_, hi_) + [rk, "prm"], writes=akeys("act", [c], s, lo_, hi_))
            rms_apply(s, lo_, hi_, x, "x", 8, 0, gcol, l, app)

        def postnorm_residual(l, s, lo_, hi_, gcol):
            def app(c, rs, rk):
                n = hi_ - lo_
                j = rr["ev"]; rr["ev"] ^= 1
                S.op("dve", lambda e: e.scalar_tensor_tensor(tmpa[j][:, 0:n], F1[:, c, lo_:hi_], prm[:, l, gcol + c:gcol + c + 1], rs, ALU.mult, ALU.mult),
                     reads=akeys("F1", [c], s, lo_, hi_) + [rk, "prm"], writes=[("tmpa", j)])
                S.op("pool", lambda e: e.tensor_tensor(x[:, c, lo_:hi_], x[:, c, lo_:hi_], tmpa[j][:, 0:n], ALU.add),
                     reads=akeys("x", [c], s, lo_, hi_) + [("tmpa", j)], writes=akeys("x", [c], s, lo_, hi_))
            rms_apply(s, lo_, hi_, F1, "F1", 8, 0, gcol, l, app)

        def norm_pair(postf, pref, tiles):
            nt = len(tiles)
            for i in range(nt + 1):
                if i < nt:
                    postf(*tiles[i])
                if i >= 1:
                    pref(*tiles[i - 1])

        def tile_outer_phase(pidxs, tiles, mm_fn, postf, pref):
            last = pidxs[-1]
            ws = [WS.get(p, ahead=min(AHEAD, last - p)) for p in pidxs]
            nt = len(tiles)
            for ti, (lo_, hi_) in enumerate(tiles):
                for half in range(2):
                    for m in range(4 * half, 4 * half + 4):
                        mm_fn(m, ws[m], lo_, hi_)
                    if half == 0 and ti >= 1:
                        postf(*tiles[ti - 1])
                    if half == 1 and ti >= 2 and pref is not None:
                        pref(*tiles[ti - 2])
            postf(*tiles[nt - 1])
            if pref is not None:
                if nt >= 2:
                    pref(*tiles[nt - 2])
                pref(*tiles[nt - 1])

        def main_passes():
          for s in range(2):
              T = TA if s == 0 else TB
              tiles = tile_list(s)
              segs = seg_bounds(s)
              col0 = 0 if s == 0 else TA
              if s == 1:
                  S.barrier()
                  S.buf.clear()
              for c in range(8):
                  for (lo_, hi_) in tiles:
                      S.dma("sp", x[:, c, lo_:hi_], xT_in[c, :, col0 + lo_:col0 + hi_], writes=akeys("x", [c], s, lo_, hi_))
              S.dma("sp", rope[:, :, 0:T], rope_in[s, :, :, 0:T], writes=["rope"])

              for l in range(DEPTH):
                  lw = wl[l % 2]
                  S.dma("pool", lw[:], w_lru[l], writes=[("wl", 0)])
                  if s == 0:
                      S.dma("sp", xs[:, :, :, 0:3], stc_in[l], writes=["xs"])
                      S.dma("sp", h0s[:], sth_in[l], writes=["h0s"])
                      S.dma("sp", sk_old[l], cache_k[l, :, 4:128, :])
                      S.dma("sp", sv_old[l], cache_v[l, :, 4:128, :])

                  if l == 0:
                      for (lo_, hi_) in tiles:
                          prenorm(l, s, lo_, hi_, 0)

                  chk("A0_%d_%d" % (s, l))
                  xb_off = (3 - 64) if s == 0 else 3
                  XB0 = 4


                  def lru_pre(k, lo_, hi_, c):
                      n = hi_ - lo_
                      uu, ub = u32[k % 3], u16[k % 3]
                      uk, ubk = ("u32", k % 3), ("u16", k % 3)
                      cw = lambda j: prm[:, l, 40 + 4 * c + j:41 + 4 * c + j]
                      p0 = 0
                      if s == 0 and lo_ == 0:
                          uo = uu[:, 0:NSMP].rearrange("p (b t) -> p b t", t=4)
                          S.op("dve", lambda e: e.tensor_scalar(uo, xs[:, c, :, 0:4], cw(0), prm[:, l, 56 + c:57 + c], ALU.mult, ALU.add),
                               reads=["xs", "prm"], writes=[uk])
                          for j in range(1, 4):
                              S.op("dve", lambda e, j=j: e.scalar_tensor_tensor(uo, xs[:, c, :, j:j + 4], cw(j), uo, ALU.mult, ALU.add),
                                   reads=["xs", "prm", uk], writes=[uk])
                          p0 = NSMP
                      xl = lo_ + p0 + xb_off
                      npr = n - p0
                      xk = akeys("F1", [XB0 + c], s, xl - 3, xl + npr)
                      S.op("act", lambda e: e.activation(uu[:, p0:n], F1[:, XB0 + c, xl - 3:xl - 3 + npr], AF.Identity, scale=cw(0), bias=prm[:, l, 56 + c:57 + c]),
                           reads=xk + ["prm"], writes=[uk])
                      for j in range(1, 4):
                          S.op("dve", lambda e, j=j: e.scalar_tensor_tensor(uu[:, p0:n], F1[:, XB0 + c, xl - 3 + j:xl - 3 + j + npr], cw(j), uu[:, p0:n], ALU.mult, ALU.add),
                               reads=xk + ["prm", uk], writes=[uk])

                  def lru_cast(k, lo_, hi_, c):
                      n = hi_ - lo_
                      S.op("act", lambda e: e.copy(u16[k % 3][:, 0:n], u32[k % 3][:, 0:n]), reads=[("u32", k % 3)], writes=[("u16", k % 3)])

                  def lru_A(k, lo_, hi_, c):
                      n = hi_ - lo_
                      lt1, lt2, lt3 = lt[k % 2]
                      k1, k2, k3 = ("lt1", k % 2), ("lt2", k % 2), ("lt3", k % 2)
                      ub, ubk = u16[k % 3], ("u16", k % 3)
                      for wi, bank in ((0, 4), (1, 5)):
                          S.op("pe", lambda e, wi=wi, bank=bank: e.matmul(P[bank][:, 0:n], lhsT=lw[:, wi, c, :], rhs=ub[:, 0:n], start=True, stop=True),
                               reads=[("wl", 0), ubk], writes=[pk(bank)])
                      S.op("act", lambda e: e.activation(lt1[:, 0:n], P[4][:, 0:n], AF.Tanh, scale=0.5, bias=c8[:, l, 16 + c:17 + c]),
                           reads=[pk(4), "c8"], writes=[k1])
                      S.op("act", lambda e: e.activation(lt2[:, 0:n], P[5][:, 0:n], AF.Tanh, scale=0.5, bias=c8[:, l, 20 + c:21 + c]),
                           reads=[pk(5), "c8"], writes=[k2])
                      S.op("act", lambda e: e.activation(lt3[:, 0:n], lt1[:, 0:n], AF.Exp, scale=c8[:, l, 8 + c:9 + c], bias=c8[:, l, 8 + c:9 + c]),
                           reads=[k1, "c8"], writes=[k3])
                      if k + 1 < len(lsteps):
                          lru_cast(k + 1, *lsteps[k + 1])
                      S.op("dve", lambda e: e.scalar_tensor_tensor(lt1[:, 0:n], lt3[:, 0:n], 1.0, lt3[:, 0:n], ALU.min, ALU.mult),
                           reads=[k3, k1], writes=[k1])
                      S.op("act", lambda e: e.activation(lt1[:, 0:n], lt1[:, 0:n], AF.Sqrt, scale=-0.25, bias=q25[:]),
                           reads=[k1, "q25"], writes=[k1])

                  def lru_B(k, lo_, hi_, c):
                      n = hi_ - lo_
                      lt1, lt2, lt3 = lt[k % 2]
                      k1, k2, k3 = ("lt1", k % 2), ("lt2", k % 2), ("lt3", k % 2)
                      uu, uk = u32[k % 3], ("u32", k % 3)
                      S.op("dve", lambda e: e.scalar_tensor_tensor(lt2[:, 0:n], lt2[:, 0:n], 1.0, uu[:, 0:n], ALU.add, ALU.mult),
                           reads=[k2, uk], writes=[k2])
                      S.op("pool", lambda e: e.tensor_tensor(lt1[:, 0:n], lt1[:, 0:n], lt2[:, 0:n], ALU.mult), reads=[k1, k2], writes=[k1])
                      if s == 0 and lo_ == 0:
                          a0 = lt3[:, 0:NSMP].rearrange("p (b t) -> p b t", t=4)[:, :, 0]
                          b0_ = lt1[:, 0:NSMP].rearrange("p (b t) -> p b t", t=4)[:, :, 0]
                          S.op("dve", lambda e: e.tensor_tensor(a0, a0, h0s[:, c, :], ALU.mult), reads=[k3, "h0s"], writes=[k3])
                          S.op("dve", lambda e: e.tensor_tensor(b0_, b0_, a0, ALU.add), reads=[k1, k3], writes=[k1])
                          S.op("dve", lambda e: e.memset(a0, 0.0), reads=[k1], writes=[k3])
                          S.op("dve", lambda e: e.memset(lt3[:, NSMP:NSMP + 1], 0.0), writes=[k3])
                          init = 0.0
                          ik = []
                      elif s == 1 and lo_ == 0:
                          init = h_c[:, l, c:c + 1]
                          ik = ["h_c"]
                      else:
                          init = hstate[c]
                          ik = ["hlast%d" % c]
                      S.op("dve", lambda e: e.tensor_tensor_scan(lt2[:, 0:n], lt3[:, 0:n], lt1[:, 0:n], init, ALU.mult, ALU.add),
                           reads=[k3, k1, k2] + ik, writes=[k2])
                      hl = sb_h[c]
                      S.op("pool", lambda e: e.tensor_copy(hl[:, 0:1], lt2[:, n - 1:n]), reads=[k2], writes=["hlast%d" % c])
                      hstate[c] = hl[:, 0:1]
                      if s == 0 and lo_ == 0:
                          S.op("pool", lambda e: e.tensor_copy(hfin[:, c, :], lt2[:, 0:NSMP].rearrange("p (b t) -> p b t", t=4)[:, :, 3]),
                               reads=[k2], writes=["hfin"])
                      S.op("pool", lambda e: e.tensor_tensor(F1[:, c, lo_:hi_], lt2[:, 0:n], F1[:, c, lo_:hi_], ALU.mult),
                           reads=[k2] + akeys("F1", [c], s, lo_, hi_), writes=akeys("F1", [c], s, lo_, hi_))

                  def lru_norm(lo_, hi_):
                      def app(c, rs, rk):
                          S.op("dve", lambda e: e.scalar_tensor_tensor(act[:, 4 + c, lo_:hi_], F1[:, c, lo_:hi_], prm[:, l, 36 + c:37 + c], rs, ALU.mult, ALU.mult),
                               reads=akeys("F1", [c], s, lo_, hi_) + [rk, "prm"], writes=akeys("act", [4 + c], s, lo_, hi_))
                      rms_apply(s, lo_, hi_, F1, "F1", 4, 0, 36, l, app, sq_eng="pool")

                  if l == 0 and s == 0:
                      sb_h = [sb("hl%d" % c, [128, 1]) for c in range(4)]
                      hfin = sb("hfin", [128, 4, NSEQ])
                      hfinp = sb("hfinp", [128, 4])
                      cfin = sb("cfin", [128, 4, 3 * NSEQ])
                      cfinp = sb("cfinp", [128, 4, 3])

                  hstate = {}
                  lsteps = [(lo_, hi_, c) for (lo_, hi_) in tiles for c in range(4)]
                  lstate = {"k": 0}

                  def lru_advance(nsteps):
                      for _ in range(nsteps):
                          k = lstate["k"]
                          if k >= len(lsteps):
                              return
                          lo_, hi_, c = lsteps[k]
                          if k + 2 < len(lsteps):
                              lru_pre(k + 2, *lsteps[k + 2])
                          if k + 1 < len(lsteps):
                              lru_A(k + 1, *lsteps[k + 1])
                          lru_B(k, lo_, hi_, c)
                          lstate["k"] = k + 1

                  def inproj_fm(uidx):
                      w, wk = WS.get(plan_idx[("in", s, l, uidx)])
                      res = []
                      for (lo_, hi_) in tiles:
                          n = hi_ - lo_
                          b = mm_bank()
                          mm_group(P[b][:, 0:n], pk(b),
                                   [(w[:, kc, :], act[:, kc, lo_:hi_], [wk] + akeys("act", [kc], s, lo_, hi_)) for kc in range(8)])
                          res.append((b, lo_, hi_, n))
                      return res

                  for c in range(4):
                      for (b0, lo_, hi_, n) in inproj_fm(13 + c):
                          plo = lo_
                          if s == 0 and lo_ == 0:
                              S.op("act", lambda e: e.copy(xs[:, c, :, 3:7], P[b0][:, 0:NSMP].rearrange("p (b t) -> p b t", t=4)),
                                   reads=[pk(b0)], writes=["xs"])
                              plo = NSMP
                          S.op("act", lambda e: e.copy(F1[:, XB0 + c, plo + xb_off:hi_ + xb_off], P[b0][:, plo - lo_:n]),
                               reads=[pk(b0)], writes=akeys("F1", [XB0 + c], s, plo + xb_off, hi_ + xb_off))
                  for c in range(4):
                      for (b0, lo_, hi_, n) in inproj_fm(17 + c):
                          S.op("act", lambda e: e.activation(F1[:, c, lo_:hi_], P[b0][:, 0:n], AF.Gelu_apprx_tanh),
                               reads=[pk(b0)], writes=akeys("F1", [c], s, lo_, hi_))
                  if s == 0:
                      S.op("pool", lambda e: e.memset(F1[:, XB0:XB0 + 4, 0:3], 0.0),
                           writes=akeys("F1", [XB0 + c for c in range(4)], s, 0, 3))
                  else:
                      S.op("pool", lambda e: e.tensor_copy(F1[:, XB0:XB0 + 4, 0:3], xbc[:, l, :, :]), reads=["xbc"],
                           writes=akeys("F1", [XB0 + c for c in range(4)], s, 0, 3))

                  fence(ATT_KEYS, LRU_KEYS)
                  lru_pre(0, *lsteps[0])
                  lru_cast(0, *lsteps[0])
                  lru_pre(1, *lsteps[1])
                  lru_A(0, *lsteps[0])
                  nqk = 0
                  for (u0, u1, dchunk) in [(0, 4, 0), (1, 5, 1), (2, 6, 2), (3, 7, 3), (8, 10, KT0), (9, 11, KT0 + 1)]:
                      w0, wk0 = WS.get(plan_idx[("in", s, l, u0)])
                      w1, wk1 = WS.get(plan_idx[("in", s, l, u1)])
                      for (lo_, hi_) in tiles:
                          nqk += 1
                          lru_advance((len(lsteps) * nqk) // (6 * len(tiles)) - lstate["k"])
                          n = hi_ - lo_
                          b0 = mm_bank(); b1 = mm_bank()
                          mm_group(P[b0][:, 0:n], pk(b0),
                                   [(w0[:, kc, :], act[:, kc, lo_:hi_], [wk0] + akeys("act", [kc], s, lo_, hi_)) for kc in range(8)])
                          mm_group(P[b1][:, 0:n], pk(b1),
                                   [(w1[:, kc, :], act[:, kc, lo_:hi_], [wk1] + akeys("act", [kc], s, lo_, hi_)) for kc in range(8)])
                          j = rr["ev"]; rr["ev"] ^= 1
                          S.op("dve", lambda e: e.tensor_tensor(tmpa[j][:, 0:n], P[b1][:, 0:n], rope[:, 1, lo_:hi_], ALU.mult),
                               reads=[pk(b1), "rope"], writes=[("tmpa", j)])
                          S.op("dve", lambda e: e.tensor_tensor(tmpb[j][:, 0:n], P[b0][:, 0:n], rope[:, 0, lo_:hi_], ALU.mult),
                               reads=[pk(b0), "rope"], writes=[("tmpb", j)])
                          S.op("pool", lambda e: e.tensor_tensor(H[:, dchunk, lo_:hi_], tmpb[j][:, 0:n], tmpa[j][:, 0:n], ALU.add),
                               reads=[("tmpb", j), ("tmpa", j)], writes=akeys("H", [dchunk], s, lo_, hi_))
                          if dchunk >= KT0:
                              g = dchunk - KT0
                              if s == 0 and lo_ == 0:
                                  S.op("pool", lambda e: e.tensor_tensor(kv32[:, g, 0:NSMP], tmpb[j][:, 0:NSMP], tmpa[j][:, 0:NSMP], ALU.add),
                                       reads=[("tmpb", j), ("tmpa", j)], writes=[("kv32", g, 0)])
                              if s == 1 and hi_ == TB:
                                  S.op("pool", lambda e: e.tensor_tensor(kv32[:, g, NSMP:NSMP + 128], tmpb[j][:, n - 128:n], tmpa[j][:, n - 128:n], ALU.add),
                                       reads=[("tmpb", j), ("tmpa", j)], writes=[("kv32", g, 1)])
                  w, wk = WS.get(plan_idx[("in", s, l, 12)])
                  for sgi, (a_, b_) in enumerate(segs):
                      nt = b_ - a_
                      bk = mm_bank()
                      mm_group(P[bk][0:nt, 0:128], pk(bk),
                               [(act[:, kc, a_:b_], w[:, kc, :], [wk] + akeys("act", [kc], s, a_, b_)) for kc in range(8)])
                      src = P[bk][0:nt, 0:128].rearrange("p (g d) -> p g d", g=2).unsqueeze(2).to_broadcast([nt, 2, 2, 64])
                      S.op("dve", lambda e: e.tensor_copy(vsb[0:nt, sgi, :, :].rearrange("p g (r d) -> p g r d", r=2), src),
                           reads=[pk(bk)], writes=[("vsb", sgi)] + [("H", c, sg) for c in (6, 7, 8) for sg in range(len(segs))])
                      want = (s == 0 and sgi == 0) or (s == 1 and sgi == len(segs) - 1)
                      if want:
                          slot = 0 if s == 0 else 1
                          S.op("act", lambda e: e.copy(v32[0:nt, slot, :], P[bk][0:nt, 0:128]), reads=[pk(bk)], writes=[("v32", slot)])
                          row0 = 0 if s == 0 else NSMP
                          S.dma("sp", vout[l, row0:row0 + nt, :], v32[0:nt, slot, :], reads=[("v32", slot)])
                  lru_advance(len(lsteps))
                  if s == 0:
                      for g in range(2):
                          S.dma("sp", koutT[l, g, :, 0:NSMP], kv32[:, g, 0:NSMP], reads=[("kv32", g, 0)])
                  else:
                      for g in range(2):
                          S.dma("sp", koutT[l, g, :, NSMP:NSMP + 128], kv32[:, g, NSMP:NSMP + 128], reads=[("kv32", g, 1)])

                  chk("A1_%d_%d" % (s, l))
                  def attn_p1(ui, g, q_lo, nq, kblocks, dst_oT):
                      pt = pT[ui % 3]
                      sb0 = 0
                      nkb = len(kblocks)
                      qk = akeys("H", [2 * g, 2 * g + 1], s, q_lo, q_lo + nq)
                      for half in range(2):
                          ps = slice(64 * half, 64 * half + 64)
                          for kb, (kap, kk, vap, vk, mk, nk) in enumerate(kblocks):
                              outp = P[sb0 + half][0:nk, kb * 256:(kb + 1) * 256].rearrange("p (c q) -> p c q", c=2)[:, :, 0:nq]
                              S.op("pe", lambda e, kap=kap, outp=outp: e.matmul(outp, lhsT=kap[ps, :], rhs=qT[ps, 2 * g:2 * g + 2, q_lo:q_lo + nq], start=True, stop=True),
                                   reads=kk + qk, writes=[pk(sb0 + half)], inc=(kb == nkb - 1))
                      for half in range(2):
                          same = all(kbl[5] == kblocks[0][5] for kbl in kblocks)
                          if same and nkb == 2 and nq == 128:
                              nk = kblocks[0][5]
                              S.op("act", lambda e, nk=nk: e.activation(pt[0:nk, :, half, :, :].rearrange("p k c q -> p k (c q)"), P[sb0 + half][0:nk, :].rearrange("p (k x) -> p k x", k=2), AF.Exp, scale=0.125),
                                   reads=[pk(sb0 + half)], writes=[("pT", ui % 3, half)])
                          else:
                              for kb, (kap, kk, vap, vk, mk, nk) in enumerate(kblocks):
                                  inp = P[sb0 + half][0:nk, kb * 256:(kb + 1) * 256].rearrange("p (c q) -> p c q", c=2)[:, :, 0:nq]
                                  S.op("act", lambda e, inp=inp, kb=kb, nk=nk: e.activation(pt[0:nk, kb, half, :, 0:nq], inp, AF.Exp, scale=0.125),
                                       reads=[pk(sb0 + half)], writes=[("pT", ui % 3, half)])
                  def attn_p1m(ui, g, q_lo, nq, kblocks, dst_oT):
                      pt = pT[ui % 3]
                      nkb = len(kblocks)
                      same = all(kbl[5] == kblocks[0][5] for kbl in kblocks)
                      for half in range(2):
                          if same and nkb == 2 and nq == 128 and kblocks[0][5] == 128:
                              S.op("dve", lambda e: e.tensor_tensor(pt[:, :, half, :, :], pt[:, :, half, :, :],
                                                                    msk[:, 0:2, :].unsqueeze(2).to_broadcast([128, 2, 2, 128]), ALU.mult),
                                   reads=[("pT", ui % 3, half), "msk"], writes=[("pT", ui % 3, half)])
                          else:
                              for kb, (kap, kk, vap, vk, mk, nk) in enumerate(kblocks):
                                  S.op("dve", lambda e, kb=kb, nk=nk, mk=mk: e.tensor_tensor(pt[0:nk, kb, half, :, 0:nq], pt[0:nk, kb, half, :, 0:nq],
                                                                                         mk.unsqueeze(1).to_broadcast([nk, 2, nq]), ALU.mult),
                                       reads=[("pT", ui % 3, half), "msk"], writes=[("pT", ui % 3, half)])

                  def attn_p2(ui, g, q_lo, nq, kblocks, dst_oT):
                      pt = pT[ui % 3]
                      dn = den[ui % 2]
                      nkb = len(kblocks)
                      pvb, smb = (2, 3) if ui % 2 == 0 else (4, 5)
                      for which, bank in (("pv", pvb), ("sum", smb)):
                          outp = P[bank][:, :].rearrange("p (h c q) -> p h c q", h=2, c=2)[:, :, :, 0:nq]
                          for kb, (kap, kk, vap, vk, mk, nk) in enumerate(kblocks):
                              lt = vap if which == "pv" else ones[0:nk, :]
                              last = (kb == nkb - 1) and which == "pv"
                              S.op("pe", lambda e, lt=lt, outp=outp, kb=kb, nk=nk, last=last: e.matmul(outp, lhsT=lt, rhs=pt[0:nk, kb, :, :, 0:nq], start=(kb == 0), stop=last),
                                   reads=[("pT", ui % 3, 0), ("pT", ui % 3, 1)] + (vk if which == "pv" else ["ones"]), writes=[pk(bank)], inc=last)
                          if which == "sum":
                              er = esr[0:1, g, :].rearrange("p (h c q) -> p h c q", h=2, c=2)[:, :, :, 0:nq]
                              S.op("pe", lambda e, outp=outp, er=er: e.matmul(outp, lhsT=ones[0:1, :], rhs=er, start=False, stop=True),
                                   reads=["ones", "esr"], writes=[pk(bank)], inc=True)
                      dv = dn[:, :].rearrange("p (h c q) -> p h c q", h=2, c=2)[:, :, :, 0:nq]
                      sv = P[smb][:, :].rearrange("p (h c q) -> p h c q", h=2, c=2)[:, :, :, 0:nq]
                      S.op("act", lambda e: e.activation(dv, sv, AF.Ln), reads=[pk(smb)], writes=[("den", ui % 2)])
                      S.op("act", lambda e: e.activation(dv, dv, AF.Exp, scale=-1.0), reads=[("den", ui % 2)], writes=[("den", ui % 2)])
                      for half in range(2):
                          ps = slice(64 * half, 64 * half + 64)
                          pvv = P[pvb][:, :].rearrange("p (h c q) -> p h c q", h=2, c=2)
                          dvv = dn[:, :].rearrange("p (h c q) -> p h c q", h=2, c=2)
                          S.op("dve", lambda e, ps=ps, half=half: e.tensor_tensor(dst_oT[ps, 2 * g:2 * g + 2, 0:nq], pvv[ps, half, :, 0:nq], dvv[ps, half, :, 0:nq], ALU.mult),
                               reads=[pk(pvb), ("den", ui % 2)], writes=[("oT", id(dst_oT), g)])

                  def attn_norm(oTt, lo_, nq):
                      i = rr["n"]; rr["n"] = (i + 1) % 3
                      rs = rstd[i]
                      okeys = [("oT", id(oTt), 0), ("oT", id(oTt), 1)]
                      S.op("pool", lambda e: e.tensor_tensor(sqa[:, :, 0:nq], oTt[:, :, 0:nq], oTt[:, :, 0:nq], ALU.mult), reads=okeys, writes=["sqa"])
                      for c in range(4):
                          S.op("pe", lambda e, c=c: e.matmul(P[6][:, 0:nq], lhsT=ones[:], rhs=sqa[:, c, 0:nq], start=(c == 0), stop=(c == 3)),
                               reads=["sqa", "ones"], writes=[pk(6)], inc=(c == 3))
                      S.op("act", lambda e: e.activation(rs[:, 0:nq], P[6][:, 0:nq], AF.Ln, bias=epsc[:, 1:2]),
                           reads=[pk(6), "epsc"], writes=[("rstd", i)])
                      S.op("act", lambda e: e.activation(rs[:, 0:nq], rs[:, 0:nq], AF.Exp, scale=-0.5), reads=[("rstd", i)], writes=[("rstd", i)])
                      for c in range(4):
                          S.op("dve", lambda e, c=c: e.scalar_tensor_tensor(act[:, c, lo_:lo_ + nq], oTt[:, c, 0:nq], prm[:, l, 32 + c:33 + c], rs[:, 0:nq], ALU.mult, ALU.mult),
                               reads=okeys + [("rstd", i), "prm"], writes=akeys("act", [c], s, lo_, lo_ + nq))

                  def sample_attention():
                      oTt = oT[0]
                      es = esink[:, l, 0:8].unsqueeze(2).to_broadcast([128, 8, 64])
                      qk = akeys("H", [0, 1, 2, 3], s, 0, NSMP)
                      kk_new = akeys("H", [KT0, KT0 + 1], s, 0, NSMP)
                      for half in range(2):
                          ps = slice(64 * half, 64 * half + 64)
                          for g in range(2):
                              outp = P[half][0:NSMP, 256 + g * 128:256 + (g + 1) * 128].rearrange("p (c q) -> p c q", c=2)
                              S.op("pe", lambda e, outp=outp, g=g, ps=ps: e.matmul(outp, lhsT=H[ps, KT0 + g, 0:NSMP], rhs=qT[ps, 2 * g:2 * g + 2, 0:NSMP], start=True, stop=True),
                                   reads=kk_new + qk, writes=[pk(half), ("Pn", half)])
                      def load_caches(grp):
                          b0 = 4 * grp
                          kcA_, kcB_, vc_ = kcA[grp % 2], kcB[grp % 2], vc[grp % 2]
                          kAk, kBk, vck = ("kcA", 0), ("kcB", 0), ("vc", 0)
                          srck = cache_k[l, b0:b0 + 4].rearrange("b j x -> j b x")
                          srcv = cache_v[l, b0:b0 + 4].rearrange("b j (g d) -> j b g d", g=2)
                          S.dma("pool", kcA_, srck, writes=[kAk])
                          for g in range(2):
                              S.dma("pool", vc_[:, :, g, 0, :], srcv[:, :, g, :], writes=[vck])
                          S.op("dve", lambda e: e.tensor_copy(kcB_[:, :, 0:64], kcA_[:, :, 64:128]), reads=[kAk], writes=[kBk])
                          S.op("dve", lambda e: e.tensor_copy(kcB_[:, :, 64:128], kcA_[:, :, 0:64]), reads=[kAk, kBk], writes=[kBk])
                          S.op("dve", lambda e: e.tensor_copy(vc_[:, :, :, 1, :], vc_[:, :, :, 0, :]), reads=[vck], writes=[vck])
                      for grp in range(4):
                          load_caches(grp)
                          b0 = 4 * grp
                          kcA_, kcB_, kTc_, vc_ = kcA[grp % 2], kcB[grp % 2], kTc[grp % 2], vc[grp % 2]
                          kAk, kBk, kTk, vck = ("kcA", 0), ("kcB", 0), ("kTc", 0), ("vc", 0)
                          for bb in range(4):
                              for ab, src in ((0, kcA_), (1, kcB_)):
                                  S.op("pe", lambda e, ab=ab, src=src, bb=bb: e.transpose(PT[:, (bb * 2 + ab) * 128:(bb * 2 + ab + 1) * 128], src[:, bb, :], ident[:]),
                                       reads=[kAk, kBk, "ident"], writes=[pk(7)], inc=(bb == 3 and ab == 1))
                          S.op("act", lambda e: e.copy(kTc_.rearrange("p b a j -> p (b a j)"), PT[:, :]),
                               reads=[pk(7)], writes=[kTk])
                          for half in range(2):
                              ps = slice(64 * half, 64 * half + 64)
                              for b in range(4):
                                  for g in range(2):
                                      ab = 0 if g == half else 1
                                      bg = b0 + b
                                      outp = P[half][:, g * 128:(g + 1) * 128].rearrange("p (c b t) -> p c b t", c=2, b=NSEQ)[:, :, bg, :]
                                      last = (b == 3 and g == 1)
                                      S.op("pe", lambda e, outp=outp, b=b, g=g, ab=ab, ps=ps, bg=bg: e.matmul(outp, lhsT=kTc_[ps, b, ab, :], rhs=qT[ps, 2 * g:2 * g + 2, 4 * bg:4 * bg + 4], start=True, stop=True),
                                           reads=[kTk] + qk, writes=[pk(half), ("Pc", half, grp)], inc=last)
                          for half in range(2):
                              inp = P[half][:, 0:256].rearrange("p (x b t) -> p x b t", x=4, b=NSEQ)[:, :, b0:b0 + 4, :]
                              outp = pTc[:, half * 256:(half + 1) * 256].rearrange("p (x b t) -> p x b t", x=4, b=NSEQ)[:, :, b0:b0 + 4, :]
                              S.op("act", lambda e, inp=inp, outp=outp: e.activation(outp, inp, AF.Exp, scale=0.125),
                                   reads=[pk(half), ("Pc", half, grp)], writes=[("pTc", half, grp)])
                              S.op("dve", lambda e, outp=outp: e.tensor_tensor(outp, outp, msk[:, 3, 0:4].unsqueeze(1).unsqueeze(1).to_broadcast([128, 4, 4, 4]), ALU.mult),
                                   reads=[("pTc", half, grp), "msk"], writes=[("pTc", half, grp)])
                          for b in range(4):
                              for g in range(2):
                                  bg = b0 + b
                                  rh = pTc[:, :].rearrange("p (h g c b t) -> p h g c b t", h=2, g=2, c=2, b=NSEQ)[:, :, g, :, bg, :]
                                  outp = P[2][:, :].rearrange("p (h g c b t) -> p h g c b t", h=2, g=2, c=2, b=NSEQ)[:, :, g, :, bg, :]
                                  S.op("pe", lambda e, rh=rh, outp=outp, b=b, g=g: e.matmul(outp, lhsT=vc_[:, b, g, :, :].rearrange("p r d -> p (r d)"), rhs=rh, start=True, stop=True),
                                       reads=[vck, ("pTc", 0, grp), ("pTc", 1, grp)], writes=[pk(2)], inc=(b == 3 and g == 1))
                      for half in range(2):
                          S.op("act", lambda e, half=half: e.activation(pTn[0:NSMP, half * 256:(half + 1) * 256], P[half][0:NSMP, 256:512], AF.Exp, scale=0.125),
                               reads=[pk(half), ("Pn", half)], writes=[("pTn", half)])
                          o2 = pTn[0:NSMP, half * 256:(half + 1) * 256].rearrange("p (x q) -> p x q", x=4)
                          S.op("dve", lambda e, o2=o2: e.tensor_tensor(o2, o2, msk[0:NSMP, 4, 0:64].unsqueeze(1).to_broadcast([NSMP, 4, 64]), ALU.mult),
                               reads=[("pTn", half), "msk"], writes=[("pTn", half)])
                      for g in range(2):
                          rh = pTn[0:NSMP, :].rearrange("p (h g x) -> p h g x", h=2, g=2)[:, :, g, :]
                          outp = P[6][:, :].rearrange("p (h g x) -> p h g x", h=2, g=2)[:, :, g, :]
                          S.op("pe", lambda e, rh=rh, outp=outp, g=g: e.matmul(outp, lhsT=vsb[0:NSMP, 0, g, :], rhs=rh, start=True, stop=True),
                               reads=[("vsb", 0), ("pTn", 0), ("pTn", 1)], writes=[pk(6)], inc=(g == 1))
                      S.op("pe", lambda e: e.matmul(P[3][:, :], lhsT=ones[:], rhs=pTc[:, :], start=True, stop=False),
                           reads=["ones"] + [("pTc", h_, g_) for h_ in range(2) for g_ in range(4)], writes=[pk(3)], inc=False)
                      S.op("pe", lambda e: e.matmul(P[3][:, :], lhsT=ones[0:NSMP, :], rhs=pTn[0:NSMP, :], start=False, stop=True),
                           reads=["ones", ("pTn", 0), ("pTn", 1)], writes=[pk(3)])
                      dn = den[0]
                      S.op("dve", lambda e: e.tensor_tensor(dn[:, :].rearrange("p (x q) -> p x q", x=8), P[3][:, :].rearrange("p (x q) -> p x q", x=8), es, ALU.add),
                           reads=[pk(3), "esink"], writes=[("den", 0)])
                      S.op("act", lambda e: e.activation(dn[:, :], dn[:, :], AF.Ln), reads=[("den", 0)], writes=[("den", 0)])
                      S.op("act", lambda e: e.activation(dn[:, :], dn[:, :], AF.Exp, scale=-1.0), reads=[("den", 0)], writes=[("den", 0)])
                      S.op("dve", lambda e: e.tensor_copy(den[1][:, :], P[2][:, :]), reads=[pk(2)], writes=[("den", 1)])
                      S.op("dve", lambda e: e.tensor_tensor(den[1][:, :], den[1][:, :], P[6][:, :], ALU.add), reads=[pk(6), ("den", 1)], writes=[("den", 1)])
                      for half in range(2):
                          ps = slice(64 * half, 64 * half + 64)
                          nv = den[1][:, :].rearrange("p (h x q) -> p h x q", h=2, x=4)
                          dv = dn[:, :].rearrange("p (h x q) -> p h x q", h=2, x=4)
                          S.op("dve", lambda e, ps=ps, half=half: e.tensor_tensor(oTt[ps, :, 0:64], nv[ps, half, :, :], dv[ps, half, :, :], ALU.mult),
                               reads=[("den", 0), ("den", 1)], writes=[("oT", id(oTt), 0), ("oT", id(oTt), 1)])
                      attn_norm(oTt, 0, NSMP)

                  fence(LRU_KEYS, ATT_KEYS)
                  S.op("dve", lambda e: e.tensor_copy(esr.rearrange("p g (x q) -> p g x q", x=4),
                                                      esink[0:1, l, 8:16].rearrange("p (g x) -> p g x", g=2).unsqueeze(3).to_broadcast([1, 2, 4, 128])),
                       reads=["esink"], writes=["esr"])
                  for (lo_, hi_) in tiles:
                      lru_norm(lo_, hi_)
                  units = []
                  if s == 0:
                      sample_attention()
                      for g in range(2):
                          kbm = [(H[:, KT0 + g, 64:80], akeys("H", [KT0 + g], s, 64, 80), vsb[0:16, 1, g, :], [("vsb", 1)], msk[0:16, 1, 0:16], 16)]
                          units.append((g, 64, 16, kbm, oT[1], g == 1))
                  blocks = list(range(2, 10)) if s == 0 else list(range(0, 8))
                  for bi, sgi in enumerate(blocks):
                      a_, b_ = segs[sgi]
                      oTt = oT[(bi + 2) % 3]
                      for g in range(2):
                          cur = (H[:, KT0 + g, a_:b_], akeys("H", [KT0 + g], s, a_, b_), vsb[:, sgi, g, :], [("vsb", sgi)], msk[:, 1, :], 128)
                          if s == 0 and sgi == 2:
                              prev = (H[:, KT0 + g, 64:80], akeys("H", [KT0 + g], s, 64, 80), vsb[0:16, 1, g, :], [("vsb", 1)], msk[0:16, 2, :], 16)
                          elif s == 1 and sgi == 0:
                              prev = (kT_c[:, l, g, :], ["kT_c"], v_c[:, l, g, :], ["v_c"], msk[:, 0, :], 128)
                          else:
                              pa, pb = segs[sgi - 1]
                              prev = (H[:, KT0 + g, pa:pb], akeys("H", [KT0 + g], s, pa, pb), vsb[:, sgi - 1, g, :], [("vsb", sgi - 1)], msk[:, 0, :], 128)
                          units.append((g, a_, 128, [prev, cur], oTt, g == 1))
                  nun = len(units)
                  pend = []
                  for j0 in range(min(2, nun)):
                      attn_p1(j0, *units[j0][:5])
                      attn_p1m(j0, *units[j0][:5])
                  for i in range(nun):
                      if i + 2 < nun:
                          attn_p1(i + 2, *units[i + 2][:5])
                          attn_p1m(i + 2, *units[i + 2][:5])
                      attn_p2(i, *units[i][:5])
                      for _f in range(NFILL):
                          S.op("pe", lambda e: e.matmul(P[7][:, :], lhsT=ones[:], rhs=msk[:, 0:4, :].rearrange("p a q -> p (a q)"), start=True, stop=True),
                               reads=["ones", "msk"], writes=[pk(7)], inc=False)
                      if pend and pend[0][3] <= i - 3:
                          attn_norm(*pend.pop(0)[:3])
                      if units[i][5]:
                          pend.append((units[i][4], units[i][1], units[i][2], i))
                  for pn in pend:
                      attn_norm(*pn[:3])

                  if s == 0:
                      la, lb = segs[-1]
                      S.op("pool", lambda e: e.tensor_copy(kT_c[:, l, :, :], H[:, KT0:KT0 + 2, la:lb]), reads=akeys("H", [KT0, KT0 + 1], s, la, lb), writes=["kT_c"])
                      S.op("pool", lambda e: e.tensor_copy(v_c[:, l, :, :], vsb[:, 9, :, :]), reads=[("vsb", 9)], writes=["v_c"])
                      for c in range(4):
                          S.op("pool", lambda e, c=c: e.tensor_copy(h_c[:, l, c:c + 1], hstate[c]), reads=["hlast%d" % c], writes=["h_c"])
                      S.op("pool", lambda e: e.tensor_copy(cfin[:, :, :].rearrange("p c (b j) -> p c b j", j=3), xs[:, :, :, 4:7]),
                           reads=["xs"], writes=["cfin_s"])
                      S.dma("sp", cout_s[l], cfin[:, :, :], reads=["cfin_s"])
                      S.dma("sp", hout_s[l], hfin[:, :, :], reads=["hfin"])
                      S.op("pool", lambda e: e.tensor_copy(xbc[:, l, :, :], F1[:, XB0:XB0 + 4, TA + xb_off - 3:TA + xb_off]),
                           reads=akeys("F1", [XB0 + c for c in range(4)], s, TA + xb_off - 3, TA + xb_off), writes=["xbc"])
                  else:
                      for c in range(4):
                          S.op("pool", lambda e, c=c: e.tensor_copy(hfinp[:, c:c + 1], hstate[c]), reads=["hlast%d" % c], writes=["hfin_p"])
                      S.dma("sp", hout_p[l], hfinp[:, :], reads=["hfin_p"])
                      S.op("pool", lambda e: e.tensor_copy(cfinp[:, :, :], F1[:, XB0:XB0 + 4, TB + xb_off - 3:TB + xb_off]),
                           reads=akeys("F1", [XB0 + c for c in range(4)], s, TB + xb_off - 3, TB + xb_off), writes=["cfin_p"])
                      S.dma("sp", cout_p[l], cfinp[:, :, :], reads=["cfin_p"])

                  chk("A2_%d_%d" % (s, l))
                  def mm_out(m, wwk, lo_, hi_):
                      w, wk = wwk
                      n = hi_ - lo_
                      b = mm_bank()
                      mm_group(P[b][:, 0:n], pk(b),
                               [(w[:, kc, :], act[:, kc, lo_:hi_], [wk] + akeys("act", [kc], s, lo_, hi_)) for kc in range(8)])
                      if m % 2 == 0:
                          S.op("act", lambda e: e.copy(F1[:, m, lo_:hi_], P[b][:, 0:n]), reads=[pk(b)], writes=akeys("F1", [m], s, lo_, hi_))
                      else:
                          S.op("dve", lambda e: e.tensor_copy(F1[:, m, lo_:hi_], P[b][:, 0:n]), reads=[pk(b)], writes=akeys("F1", [m], s, lo_, hi_))
                  tile_outer_phase([plan_idx[("out", s, l, m)] for m in range(8)], tiles, mm_out,
                                   lambda lo_, hi_: postnorm_residual(l, s, lo_, hi_, 8),
                                   lambda lo_, hi_: prenorm(l, s, lo_, hi_, 16))

                  chk("A3_%d_%d" % (s, l))
                  for hf in range(2):
                      for c in range(11):
                          wg, wgk = WS.get(plan_idx[("g", s, l, hf, c)])
                          wu, wuk = WS.get(plan_idx[("u", s, l, hf, c)])
                          for (lo_, hi_) in tiles:
                              n = hi_ - lo_
                              bg_ = mm_bank(); bu_ = mm_bank()
                              mm_group(P[bg_][:, 0:n], pk(bg_),
                                       [(wg[:, kc, :], act[:, kc, lo_:hi_], [wgk] + akeys("act", [kc], s, lo_, hi_)) for kc in range(8)])
                              mm_group(P[bu_][:, 0:n], pk(bu_),
                                       [(wu[:, kc, :], act[:, kc, lo_:hi_], [wuk] + akeys("act", [kc], s, lo_, hi_)) for kc in range(8)])
                              j = rr["ev"]; rr["ev"] ^= 1
                              S.op("act", lambda e: e.activation(tmpb[j][:, 0:n], P[bg_][:, 0:n], AF.Silu), reads=[pk(bg_)], writes=[("tmpb", j)])
                              extra = [("vsb", sg) for sg in range(10)] if c in (6, 7, 8) else []
                              S.op("dve", lambda e: e.tensor_tensor(H[:, c, lo_:hi_], P[bu_][:, 0:n], tmpb[j][:, 0:n], ALU.mult),
                                   reads=[pk(bu_), ("tmpb", j)], writes=akeys("H", [c], s, lo_, hi_) + extra)
                      def mm_down(m, wwk, lo_, hi_, hf=hf):
                          w, wk = wwk
                          n = hi_ - lo_
                          b = mm_bank()
                          mm_group(P[b][:, 0:n], pk(b),
                                   [(w[:, kc, :], H[:, kc, lo_:hi_], [wk] + akeys("H", [kc], s, lo_, hi_)) for kc in range(11)])
                          fk = akeys("F1", [m], s, lo_, hi_)
                          if hf == 0:
                              S.op("act", lambda e: e.copy(F1[:, m, lo_:hi_], P[b][:, 0:n]), reads=[pk(b)], writes=fk)
                          else:
                              S.op("dve", lambda e: e.tensor_tensor(F1[:, m, lo_:hi_], P[b][:, 0:n], F1[:, m, lo_:hi_], ALU.add), reads=[pk(b)] + fk, writes=fk)
                      if hf == 0:
                          for m in range(8):
                              wwk = WS.get(plan_idx[("d", s, l, hf, m)])
                              for (lo_, hi_) in tiles:
                                  mm_down(m, wwk, lo_, hi_)
                      else:
                          tile_outer_phase([plan_idx[("d", s, l, 1, m)] for m in range(8)], tiles, mm_down,
                                           lambda lo_, hi_: postnorm_residual(l, s, lo_, hi_, 24),
                                           (lambda lo_, hi_: prenorm(l + 1, s, lo_, hi_, 0)) if l < DEPTH - 1 else None)

                  chk("B_%d_%d" % (s, l))
              for c in range(8):
                  if s == 0:
                      S.dma("sp", yT[c, :, 0:NSMP], x[:, c, 0:NSMP], reads=akeys("x", [c], s, 0, NSMP))
                      S.dma("sp", yT[c, :, NSMP:NSMP + 1024], x[:, c, 80:TA], reads=akeys("x", [c], s, 80, TA))
                  else:
                      S.dma("sp", yT[c, :, NSMP + 1024:NSMP + 2048], x[:, c, 0:TB], reads=akeys("x", [c], s, 0, TB))
        try:
            chk("setup")
            main_passes()
        except _Stop:
            pass
        S.finish("sp")
    return nc


_CACHE = {}


def _host_consts():
    half = 32
    inv = (np.float32(10000.0) ** (-np.arange(half, dtype=np.float32) / np.float32(half))).astype(np.float32)
    p = np.arange(128)
    d = p % 64
    fi = d % 32
    sign = np.where(d < 32, -1.0, 1.0).astype(np.float32)
    rope = np.zeros((2, 128, 2, TMAX), np.float32)
    posA = np.concatenate([np.tile(PAST + np.arange(4), NSEQ), np.arange(16), 16 + np.arange(1024)])
    posB = 16 + 1024 + np.arange(1024)
    for si, pos in enumerate([posA, posB]):
        ang = (pos.astype(np.float32)[None, :] * inv[fi][:, None]).astype(np.float32)
        a64 = ang.astype(np.float64)
        rope[si, :, 0, :len(pos)] = np.cos(a64).astype(np.float32)
        rope[si, :, 1, :len(pos)] = (np.sin(a64) * sign[:, None]).astype(np.float32)
    j = np.arange(128)[:, None]
    i = np.arange(128)[None, :]
    msk = np.zeros((128, 5, 128), np.float32)
    msk[:, 0, :] = (j > i)
    msk[:, 1, :] = (j <= i)
    msk[:, 2, :] = ((112 + j) > i) & (j < 16)
    msk[:, 3, :] = (j > i)
    bq, tq = np.arange(64) // 4, np.arange(64) % 4
    m4 = (bq[:, None] == bq[None, :]) & (tq[:, None] <= tq[None, :])
    msk[0:64, 4, 0:64] = m4
    ident = np.eye(128, dtype=np.float32)
    return rope, msk, ident


def _prep_weights(inp):
    f = lambda a: np.ascontiguousarray(a, dtype=np.float32)
    w_in = np.asarray(inp["w_in"])
    d = np.arange(64)
    rot = np.where(d < 32, d + 32, d - 32)
    cols = []
    for c in range(4):
        cols.append(c * 128 + np.arange(128))
    for c in range(4):
        cols.append(np.concatenate([c * 128 + hh * 64 + rot for hh in range(2)]))
    for g in range(2):
        cols.append(512 + 64 * g + np.concatenate([d, d]))
    for g in range(2):
        cols.append(512 + 64 * g + np.concatenate([rot, rot]))
    cols.append(640 + np.arange(128))
    for c in range(4):
        cols.append(768 + c * 128 + np.arange(128))
    for c in range(4):
        cols.append(1280 + c * 128 + np.arange(128))
    cols = np.stack(cols)
    wi = w_in[:, :, cols.reshape(-1)]
    w_in_u = f(wi.reshape(DEPTH, 8, 128, NU_IN, 128).transpose(0, 3, 2, 1, 4))
    w_out_u = f(np.asarray(inp["w_out"]).reshape(DEPTH, 8, 128, 8, 128).transpose(0, 3, 2, 1, 4))
    w_gate_u = f(np.asarray(inp["w_gate"]).reshape(DEPTH, 8, 128, NFF, 128).transpose(0, 3, 2, 1, 4))
    w_up_u = f(np.asarray(inp["w_up"]).reshape(DEPTH, 8, 128, NFF, 128).transpose(0, 3, 2, 1, 4))
    w_down_u = f(np.asarray(inp["w_down"]).reshape(DEPTH, 2, 11, 128, 8, 128).transpose(0, 1, 4, 3, 2, 5))
    w_lru = np.zeros((DEPTH, 128, 2, 4, 128), np.float32)
    for ai, nm in enumerate(["w_a", "w_i"]):
        w = np.asarray(inp[nm])
        for c in range(4):
            for bl in range(2):
                w_lru[:, bl * 64:(bl + 1) * 64, ai, c, bl * 64:(bl + 1) * 64] = w[:, 2 * c + bl]
    prm = np.zeros((128, DEPTH, 88), np.float32)

    def fm(v, nch):
        return np.asarray(v).reshape(DEPTH, nch, 128).transpose(2, 0, 1)
    prm[:, :, 0:8] = fm(inp["pre_mix_norm"], 8)
    prm[:, :, 8:16] = fm(inp["post_mix_norm"], 8)
    prm[:, :, 16:24] = fm(inp["pre_ffn_norm"], 8)
    prm[:, :, 24:32] = fm(inp["post_ffn_norm"], 8)
    prm[:, :, 32:36] = fm(inp["attn_out_norm"], 4)
    prm[:, :, 36:40] = fm(inp["lru_out_norm"], 4)
    cw = np.asarray(inp["conv_w"]).reshape(DEPTH, 4, 4, 128)
    prm[:, :, 40:56] = cw.transpose(3, 0, 2, 1).reshape(128, DEPTH, 16)
    prm[:, :, 56:60] = fm(inp["conv_b"], 4)
    prm[:, :, 60:64] = fm(inp["b_a"], 4)
    prm[:, :, 64:68] = fm(inp["b_i"], 4)
    prm[:, :, 68:72] = fm(inp["lam"], 4)
    sk = np.asarray(inp["sinks"])
    so = [4 * g + 2 * cp + half for half in range(2) for g in range(2) for cp in range(2)]
    po = [4 * g + 2 * cp + half for g in range(2) for half in range(2) for cp in range(2)]
    prm[:, :, 72:80] = sk[:, so][None]
    prm[:, :, 80:88] = sk[:, po][None]
    return dict(w_in_u=w_in_u, w_out_u=w_out_u, w_gate_u=w_gate_u, w_up_u=w_up_u,
                w_down_u=w_down_u, w_lru=w_lru, prm=prm)


def kernel(**inputs):
    inp = {k: np.asarray(v) for k, v in inputs.items()}
    if "nc" not in _CACHE:
        _CACHE["nc"] = build_program()
    nc = _CACHE["nc"]
    rope, msk, ident = _host_consts()
    shared = _prep_weights(inp)
    shared.update(rope=rope, msk=msk, ident=ident)
    f = lambda a: np.ascontiguousarray(a, dtype=np.float32)
    in_maps = []
    for c in range(NCORES):
        sq = slice(NSEQ * c, NSEQ * (c + 1))
        xa = np.concatenate([inp["x_sample"][sq].reshape(NSMP, D), inp["meta_tokens"], inp["x_prompt"][c]], axis=0)
        m = dict(shared)
        m["xT_in"] = f(xa.T.reshape(8, 128, TA + TB))
        m["cache_k"] = f(inp["cache_k"][:, sq].reshape(DEPTH, NSEQ, 128, 128))
        m["cache_v"] = f(inp["cache_v"][:, sq].reshape(DEPTH, NSEQ, 128, 128))
        m["sth"] = f(inp["state_h"][:, sq].reshape(DEPTH, NSEQ, 4, 128).transpose(0, 3, 2, 1))
        m["stc"] = f(inp["state_conv"][:, sq].reshape(DEPTH, NSEQ, 3, 4, 128).transpose(0, 4, 3, 1, 2))
        in_maps.append(m)
    res = run_bass_kernel_spmd(nc, in_maps, core_ids=list(range(NCORES)))
    R = res.results
    y_prompt = np.zeros((8, SEQ, D), np.float32)
    y_sample = np.zeros((128, 4, D), np.float32)
    prompt_k = np.zeros((DEPTH, 8, 128, 2, 64), np.float32)
    prompt_v = np.zeros((DEPTH, 8, 128, 2, 64), np.float32)
    prompt_h = np.zeros((DEPTH, 8, 512), np.float32)
    prompt_conv = np.zeros((DEPTH, 8, 3, 512), np.float32)
    sample_k = np.zeros((DEPTH, 128, 128, 2, 64), np.float32)
    sample_v = np.zeros((DEPTH, 128, 128, 2, 64), np.float32)
    sample_h = np.zeros((DEPTH, 128, 512), np.float32)
    sample_conv = np.zeros((DEPTH, 128, 3, 512), np.float32)
    for c in range(NCORES):
        r = R[c]
        sq = slice(NSEQ * c, NSEQ * (c + 1))
        yT = np.asarray(r["yT"]).reshape(D, NSMP + SEQ)
        y_prompt[c] = yT[:, NSMP:].T
        y_sample[sq] = yT[:, :NSMP].T.reshape(NSEQ, 4, D)
        kT = np.asarray(r["koutT"])[:, :, 0:64, :]
        prompt_k[:, c] = kT[:, :, :, NSMP:].transpose(0, 3, 1, 2)
        vo = np.asarray(r["vout"])
        prompt_v[:, c] = vo[:, NSMP:].reshape(DEPTH, 128, 2, 64)
        prompt_h[:, c] = np.asarray(r["hout_p"]).transpose(0, 2, 1).reshape(DEPTH, 512)
        sample_h[:, sq] = np.asarray(r["hout_s"]).transpose(0, 3, 2, 1).reshape(DEPTH, NSEQ, 512)
        prompt_conv[:, c] = np.asarray(r["cout_p"]).transpose(0, 3, 2, 1).reshape(DEPTH, 3, 512)
        sample_conv[:, sq] = np.asarray(r["cout_s"]).reshape(DEPTH, 128, 4, NSEQ, 3).transpose(0, 3, 4, 2, 1).reshape(DEPTH, NSEQ, 3, 512)
        sample_k[:, sq, 0:124] = np.asarray(r["sk_old"]).reshape(DEPTH, NSEQ, 124, 2, 64)
        sample_v[:, sq, 0:124] = np.asarray(r["sv_old"]).reshape(DEPTH, NSEQ, 124, 2, 64)
        kn = kT[:, :, :, :NSMP].reshape(DEPTH, 2, 64, NSEQ, 4)
        sample_k[:, sq, 124:128] = kn.transpose(0, 3, 4, 1, 2)
        sample_v[:, sq, 124:128] = vo[:, :NSMP].reshape(DEPTH, NSEQ, 4, 2, 64)
    return (y_prompt, y_sample, prompt_k, prompt_v, prompt_h, prompt_conv,
            sample_k, sample_v, sample_h, sample_conv)
```
